# Optimizing a Trainium2 kernel written in Bass

```python
import jax
import jax.numpy as jnp
from jax import lax
import numpy as np

D_MODEL = 1024
BATCH = 16
SEQ = 2048
DEPTH = 1

GRID_W = 64
CTX_LEN = 256
N_HEADS_NA = 8
HEAD_DIM_NA = 64
WIN_H = 8
WIN_W = 16
NA_QCOLS = 16
NA_KCOLS = 32
N_HEADS_MLA = 8
MLA_Q_RANK = 768
MLA_KV_RANK = 256
MLA_NOPE_DIM = 64
MLA_ROPE_DIM = 32
MLA_V_DIM = 64
D_FF = 2816
ROPE_BASE = 10000.0
RMS_EPS = 1e-6
ATTN_QBLOCK = 128
N_MOD = 9
HALF_STEP = 0.5
NA_WIDTH = N_HEADS_NA * HEAD_DIM_NA
MLA_QK_DIM = MLA_NOPE_DIM + MLA_ROPE_DIM
IN_SPLITS = (NA_WIDTH, 2 * NA_WIDTH, 3 * NA_WIDTH, 3 * NA_WIDTH + MLA_Q_RANK,
             3 * NA_WIDTH + MLA_Q_RANK + MLA_KV_RANK,
             3 * NA_WIDTH + MLA_Q_RANK + MLA_KV_RANK + MLA_ROPE_DIM)
IN_COLS = IN_SPLITS[-1] + 2 * D_MODEL

kernel_name = 'hybrid_natten_mla_macaron_block'


def _rmsnorm(x, g):
    x32 = x.astype(jnp.float32)
    y = x32 * lax.rsqrt(jnp.mean(x32 * x32, axis=-1, keepdims=True) + RMS_EPS)
    return y.astype(x.dtype) * g


def _modnorm(x, g_pre, shift, scale):
    return _rmsnorm(x, g_pre) * (1 + scale) + shift


def _residual(x, y, g_post, gate, weight):
    return x + weight * gate * _rmsnorm(y, g_post)


def _swiglu(h, w1, w3, w2):
    return (jax.nn.silu(h @ w1) * (h @ w3)) @ w2


def _axial_angles(n_tokens):
    half = MLA_ROPE_DIM // 4
    freqs = ROPE_BASE ** (-jnp.arange(half, dtype=jnp.float32) / half)
    t = jnp.arange(n_tokens)
    rows = (t // GRID_W).astype(jnp.float32)
    cols = (t % GRID_W).astype(jnp.float32)
    return rows[:, None] * freqs, cols[:, None] * freqs


def _rope_axis(x, ang):
    half = x.shape[-1] // 2
    cos = jnp.cos(ang)[None, :, None, :].astype(x.dtype)
    sin = jnp.sin(ang)[None, :, None, :].astype(x.dtype)
    x1, x2 = x[..., :half], x[..., half:]
    return jnp.concatenate([x1 * cos - x2 * sin, x2 * cos + x1 * sin], axis=-1)


def _axial_rope(x, ang_r, ang_c):
    a = MLA_ROPE_DIM // 2
    return jnp.concatenate([_rope_axis(x[..., :a], ang_r), _rope_axis(x[..., a:], ang_c)], axis=-1)


def _project(h, w_in, b_gate, g_q, g_kv, w_uq, w_ukv):
    B, T, _ = h.shape
    qa, ka, va, cq, ckv, kr, gates = jnp.split(h @ w_in, IN_SPLITS, axis=-1)
    heads = lambda t, n: t.reshape(B, T, n, -1)
    q = heads(_rmsnorm(cq, g_q) @ w_uq, N_HEADS_MLA)
    kv = heads(_rmsnorm(ckv, g_kv) @ w_ukv, N_HEADS_MLA)
    g = jax.nn.sigmoid(gates + b_gate)
    return (heads(qa, N_HEADS_NA), heads(ka, N_HEADS_NA), heads(va, N_HEADS_NA),
            q[..., :MLA_NOPE_DIM], q[..., MLA_NOPE_DIM:],
            kv[..., :MLA_NOPE_DIM], kr[:, :, None, :], kv[..., MLA_NOPE_DIM:],
            g[..., :D_MODEL], g[..., D_MODEL:])


def _mla_q(q_nope, q_rope):
    return jnp.concatenate([q_nope, q_rope], axis=-1)


def _mla_k(k_nope, k_rope):
    k_rope = jnp.broadcast_to(k_rope, k_nope.shape[:-1] + (k_rope.shape[-1],))
    return jnp.concatenate([k_nope, k_rope], axis=-1)


def _attend(q, k, v):
    s = jnp.einsum('bqhd,bkhd->bhqk', q * q.shape[-1] ** -0.5, k).astype(jnp.float32)
    p = jax.nn.softmax(s, axis=-1).astype(v.dtype)
    return jnp.einsum('bhqk,bkhd->bqhd', p, v)


def _blocked_attention(q, k, v):
    B, S, H, dq = q.shape
    qb = q.reshape(B, S // ATTN_QBLOCK, ATTN_QBLOCK, H, dq).swapaxes(0, 1)
    out = lax.map(lambda qq: _attend(qq, k, v), qb)
    return out.swapaxes(0, 1).reshape(B, S, H, v.shape[-1])


def _na_column_tables():
    n_blk = GRID_W // NA_QCOLS
    j = np.arange(n_blk)
    k_start = np.clip(j * NA_QCOLS - WIN_W // 2, 0, GRID_W - NA_KCOLS)
    key_col = k_start[:, None] + np.arange(NA_KCOLS)
    q_col = j[:, None] * NA_QCOLS + np.arange(NA_QCOLS)
    w_start = np.clip(q_col - WIN_W // 2, 0, GRID_W - WIN_W)
    kc = key_col[:, None, :]
    valid = (kc >= w_start[..., None]) & (kc < w_start[..., None] + WIN_W)
    off = np.clip(kc - q_col[..., None], -(WIN_W - 1), WIN_W - 1) + (WIN_W - 1)
    return jnp.asarray(key_col, jnp.int32), jnp.asarray(valid), jnp.asarray(off, jnp.int32)


def _neighbourhood_attention(q, k, v, k_ctx, v_ctx, rpb, n_rows):
    B, S, H, d = q.shape
    kh = min(WIN_H, n_rows)
    n_blk = GRID_W // NA_QCOLS
    n_loc = kh * NA_KCOLS
    key_col, valid, col_off = _na_column_tables()
    grid = lambda t: t.reshape(B, n_rows, GRID_W, H, t.shape[-1])
    qg, kg, vg = grid(q * d ** -0.5), grid(k), grid(v)
    neg = jnp.finfo(jnp.float32).min

    def row(r):
        rs = jnp.clip(r - kh // 2, 0, n_rows - kh)
        q_r = lax.dynamic_index_in_dim(qg, r, axis=1, keepdims=False).reshape(B, n_blk, NA_QCOLS, H, d)
        k_b = jnp.moveaxis(lax.dynamic_slice_in_dim(kg, rs, kh, axis=1)[:, :, key_col], 2, 1)
        v_b = jnp.moveaxis(lax.dynamic_slice_in_dim(vg, rs, kh, axis=1)[:, :, key_col], 2, 1)
        row_off = rs + jnp.arange(kh) - r + (WIN_H - 1)
        bias = rpb[:, row_off[None, None, :, None], col_off[:, :, None, :]]
        s_loc = jnp.einsum('bjqhd,bjakhd->bhjqak', q_r, k_b).astype(jnp.float32) + bias.astype(jnp.float32)
        s_loc = jnp.where(valid[:, :, None, :], s_loc, neg).reshape(B, H, n_blk, NA_QCOLS, n_loc)
        s_ctx = jnp.einsum('bjqhd,bkhd->bhjqk', q_r, k_ctx).astype(jnp.float32)
        p = jax.nn.softmax(jnp.concatenate([s_loc, s_ctx], axis=-1), axis=-1).astype(v.dtype)
        p_loc = p[..., :n_loc].reshape(B, H, n_blk, NA_QCOLS, kh, NA_KCOLS)
        o = (jnp.einsum('bhjqak,bjakhd->bjqhd', p_loc, v_b)
             + jnp.einsum('bhjqk,bkhd->bjqhd', p[..., n_loc:], v_ctx))
        return o.reshape(B, GRID_W, H, v.shape[-1])

    out = lax.map(row, jnp.arange(n_rows))
    return jnp.moveaxis(out, 0, 1).reshape(B, S, H, v.shape[-1])


def _merge(o_na, o_mla, g_na, g_mla, w_o_na, w_o_mla, w_out):
    B, T = o_na.shape[:2]
    y = g_na * (o_na.reshape(B, T, -1) @ w_o_na) + g_mla * (o_mla.reshape(B, T, -1) @ w_o_mla)
    return y @ w_out


def setup_inputs(seed: int = 0) -> dict:
    key = jax.random.key(seed)
    ks = jax.random.split(key, 24)
    nrm = lambda k, shape, s: jax.random.normal(k, shape, jnp.float32) * s
    D = D_MODEL
    return {
        'x': nrm(ks[0], (BATCH, SEQ, D), 1.0),
        'c': nrm(ks[1], (BATCH, D), 1.0),
        'ctx': nrm(ks[2], (BATCH, CTX_LEN, D), 1.0),
        'c_ctx': nrm(ks[3], (D,), 1.0),
        'w_ada': nrm(ks[4], (DEPTH, D, N_MOD * D), 0.5 * D ** -0.5),
        'b_ada': nrm(ks[5], (DEPTH, N_MOD * D), 0.02),
        'norm_g': 1.0 + nrm(ks[6], (DEPTH, 6, D), 0.02),
        'ffn1_w1': nrm(ks[7], (DEPTH, D, D_FF), D ** -0.5),
        'ffn1_w3': nrm(ks[8], (DEPTH, D, D_FF), D ** -0.5),
        'ffn1_w2': nrm(ks[9], (DEPTH, D_FF, D), D_FF ** -0.5),
        'w_in': nrm(ks[10], (DEPTH, D, IN_COLS), D ** -0.5),
        'b_gate': nrm(ks[11], (DEPTH, 2 * D), 0.1),
        'g_q_lora': 1.0 + nrm(ks[12], (DEPTH, MLA_Q_RANK), 0.02),
        'g_kv_lora': 1.0 + nrm(ks[13], (DEPTH, MLA_KV_RANK), 0.02),
        'w_uq': nrm(ks[14], (DEPTH, MLA_Q_RANK, N_HEADS_MLA * MLA_QK_DIM), MLA_Q_RANK ** -0.5),
        'w_ukv': nrm(ks[15], (DEPTH, MLA_KV_RANK, N_HEADS_MLA * (MLA_NOPE_DIM + MLA_V_DIM)), MLA_KV_RANK ** -0.5),
        'rpb': nrm(ks[16], (DEPTH, N_HEADS_NA, 2 * WIN_H - 1, 2 * WIN_W - 1), 0.1),
        'w_o_na': nrm(ks[17], (DEPTH, NA_WIDTH, D), NA_WIDTH ** -0.5),
        'w_o_mla': nrm(ks[18], (DEPTH, N_HEADS_MLA * MLA_V_DIM, D), (N_HEADS_MLA * MLA_V_DIM) ** -0.5),
        'w_out': nrm(ks[19], (DEPTH, D, D), D ** -0.5),
        'ffn2_w1': nrm(ks[20], (DEPTH, D, D_FF), D ** -0.5),
        'ffn2_w3': nrm(ks[21], (DEPTH, D, D_FF), D ** -0.5),
        'ffn2_w2': nrm(ks[22], (DEPTH, D_FF, D), D_FF ** -0.5),
    }


def reference(x, c, ctx, c_ctx, w_ada, b_ada, norm_g, ffn1_w1, ffn1_w3, ffn1_w2, w_in, b_gate,
              g_q_lora, g_kv_lora, w_uq, w_ukv, rpb, w_o_na, w_o_mla, w_out, ffn2_w1, ffn2_w3, ffn2_w2):
    n_lat = x.shape[1]
    n_rows = n_lat // GRID_W
    ang_r, ang_c = _axial_angles(n_lat)
    h_ctx = ctx
    for l in range(DEPTH):
        last = l == DEPTH - 1
        m = jnp.split((jax.nn.silu(c) @ w_ada[l] + b_ada[l])[:, None, :], N_MOD, axis=-1)
        mc = jnp.split(jax.nn.silu(c_ctx) @ w_ada[l] + b_ada[l], N_MOD, axis=-1)
        g = norm_g[l]
        x = _residual(x, _swiglu(_modnorm(x, g[0], m[0], m[1]), ffn1_w1[l], ffn1_w3[l], ffn1_w2[l]),
                      g[1], m[2], HALF_STEP)
        h_ctx = _residual(h_ctx, _swiglu(_modnorm(h_ctx, g[0], mc[0], mc[1]), ffn1_w1[l], ffn1_w3[l], ffn1_w2[l]),
                          g[1], mc[2], HALF_STEP)
        pw = (w_in[l], b_gate[l], g_q_lora[l], g_kv_lora[l], w_uq[l], w_ukv[l])
        qa, ka, va, qn, qr, kn, kr, vb, ga, gb = _project(_modnorm(x, g[2], m[3], m[4]), *pw)
        cqa, cka, cva, cqn, cqr, ckn, ckr, cvb, cga, cgb = _project(_modnorm(h_ctx, g[2], mc[3], mc[4]), *pw)
        o_na = _neighbourhood_attention(qa, ka, va, cka, cva, rpb[l], n_rows)
        k_ctx_mla = _mla_k(ckn, ckr)
        q_mla = _mla_q(qn, _axial_rope(qr, ang_r, ang_c))
        k_mla = jnp.concatenate([k_ctx_mla, _mla_k(kn, _axial_rope(kr, ang_r, ang_c))], axis=1)
        v_mla = jnp.concatenate([cvb, vb], axis=1)
        o_mla = _blocked_attention(q_mla, k_mla, v_mla)
        y = _merge(o_na, o_mla, ga, gb, w_o_na[l], w_o_mla[l], w_out[l])
        x = _residual(x, y, g[3], m[5], 1.0)
        if not last:
            yc = _merge(_attend(cqa, cka, cva), _attend(_mla_q(cqn, cqr), k_ctx_mla, cvb),
                        cga, cgb, w_o_na[l], w_o_mla[l], w_out[l])
            h_ctx = _residual(h_ctx, yc, g[3], mc[5], 1.0)
            h_ctx = _residual(h_ctx, _swiglu(_modnorm(h_ctx, g[4], mc[6], mc[7]), ffn2_w1[l], ffn2_w3[l], ffn2_w2[l]),
                              g[5], mc[8], HALF_STEP)
        x = _residual(x, _swiglu(_modnorm(x, g[4], m[6], m[7]), ffn2_w1[l], ffn2_w3[l], ffn2_w2[l]),
                      g[5], m[8], HALF_STEP)
    return x
```

```python
import contextlib
import numpy as np
import concourse.bass as bass
import concourse.mybir as mybir
from concourse.bass_utils import run_bass_kernel_spmd

F32 = mybir.dt.float32
BF16 = mybir.dt.bfloat16
AF = mybir.ActivationFunctionType
ALU = mybir.AluOpType

ENGS = ("pe", "act", "dve", "pool", "sp")
NCORES = 8
NB = 2
D = 1024
FF = 2816
SEQ = 2048
CTX = 256
NTOK = SEQ + CTX
EPS = 1e-6
NEG = -30000.0
import os
LORA3 = os.environ.get("K_LORA3", "1") == "1"
ADABG = os.environ.get("K_ADABG", "0") == "1"


class Op:
    __slots__ = ("eng", "fn", "deps", "idx", "needed", "sigval", "chan")

    def __init__(self, eng, fn, chan=None):
        self.eng = eng
        self.fn = fn
        self.deps = {}
        self.needed = False
        self.sigval = 0
        self.chan = chan
        self.idx = -1


class Prog:
    def __init__(self, nc):
        self.nc = nc
        self.ops = {e: [] for e in ENGS}
        self.res_w = {}
        self.res_r = {}
        self.pending = {e: {} for e in ENGS}
        self.chan_last = {}
        self.chans = []

    @staticmethod
    def _key(op):
        return op.chan if op.chan is not None else op.eng

    def _adddep(self, op, d, kind):
        if d is op:
            return
        if d.chan is None and d.eng == op.eng and op.chan is None:
            if op.eng == "pe" or (kind == "R" and op.eng != "pool"):
                return
        k = self._key(d)
        cur = op.deps.get(k)
        if cur is None or cur.idx < d.idx:
            op.deps[k] = d

    def add(self, eng, fn, reads=(), writes=(), chan=None):
        op = Op(eng, fn, chan)
        lst = self.ops[eng]
        if chan is not None:
            if chan not in self.chan_last:
                self.chans.append(chan)
            prev = self.chan_last.get(chan)
            op.idx = (prev.idx + 1) if prev is not None else 1
            self.chan_last[chan] = op
        else:
            op.idx = len(lst)
        for r in reads:
            w = self.res_w.get(r)
            if w is not None:
                self._adddep(op, w, "W")
        for w_ in writes:
            lw = self.res_w.get(w_)
            if lw is not None:
                self._adddep(op, lw, "W")
            rd = self.res_r.get(w_)
            if rd:
                for d in rd.values():
                    self._adddep(op, d, "R")
        for k, d in self.pending[eng].items():
            if d is not op:
                cur = op.deps.get(k)
                if cur is None or cur.idx < d.idx:
                    op.deps[k] = d
        self.pending[eng] = {}
        for d in op.deps.values():
            d.needed = True
        for r in reads:
            self.res_r.setdefault(r, {})[self._key(op)] = op
        for w_ in writes:
            self.res_w[w_] = op
            self.res_r[w_] = {}
        lst.append(op)
        return op

    def barrier(self):
        last = {}
        for e in ENGS:
            for op in reversed(self.ops[e]):
                if op.chan is None:
                    last[e] = op
                    break
        for c, op in self.chan_last.items():
            last[c] = op
        for e in ENGS:
            for k, d in last.items():
                if k == e and e != "pool":
                    continue
                cur = self.pending[e].get(k)
                if cur is None or cur.idx < d.idx:
                    self.pending[e][k] = d

    def emit(self, final_waits_eng="sp"):
        nc = self.nc
        with contextlib.ExitStack() as st:
            sems = {}
            for e in ENGS:
                sems[e] = st.enter_context(nc.semaphore("s_" + e))
            for c in self.chans:
                sems[c] = st.enter_context(nc.semaphore("c_" + str(c)))
            for e in ENGS:
                cnt = 0
                for op in self.ops[e]:
                    if op.chan is None and op.needed:
                        cnt += 1
                        op.sigval = cnt
                    elif op.chan is not None:
                        op.sigval = 16 * op.idx
            print("semaphores used:", len(sems))
            block = st.enter_context(nc.Block())
            handles = {"pe": "tensor", "act": "scalar", "dve": "vector", "pool": "gpsimd", "sp": "sync"}

            def make(e):
                def body(eng):
                    waited = {}
                    for op in self.ops[e]:
                        for k, d in op.deps.items():
                            v = d.sigval
                            assert v > 0, (e, k, d.eng, d.chan, d.idx)
                            if waited.get(k, 0) >= v:
                                continue
                            waited[k] = v
                            eng.wait_ge(sems[k], v)
                        ins = op.fn(eng)
                        if op.chan is not None:
                            ins.then_inc(sems[op.chan], 16)
                        elif op.needed:
                            ins.then_inc(sems[e], 1)
                    if e == final_waits_eng:
                        for c, op in self.chan_last.items():
                            v = op.sigval
                            if waited.get(c, 0) >= v:
                                continue
                            eng.wait_ge(sems[c], v)
                return body

            for e in ENGS:
                if not self.ops[e] and e != final_waits_eng:
                    continue
                getattr(block, handles[e])(make(e))


class Arena:
    def __init__(self, ar, nbytes):
        self.ar = ar
        self.nbytes = nbytes
        self.off = 0
        self.peak = 0
        self.top = nbytes

    def alloc(self, shape, dtype):
        esz = 4 if dtype == F32 else 2
        n = 1
        for s in shape[1:]:
            n *= s
        size = (n * esz + 63) // 64 * 64
        assert self.off + size <= min(self.nbytes, self.top if self.off < self.top else self.nbytes), ("SBUF arena overflow", self.off, size, self.nbytes, self.top)
        a = self.ar[:, self.off // 4:(self.off + size) // 4]
        if dtype != F32:
            a = a.bitcast(dtype)
        a = a[:, 0:n]
        if len(shape) == 3:
            a = a.rearrange("p (a b) -> p a b", a=shape[1])
        elif len(shape) == 4:
            a = a.rearrange("p (a b c) -> p a b c", a=shape[1], b=shape[2])
        elif len(shape) == 5:
            a = a.rearrange("p (a b c d) -> p a b c d", a=shape[1], b=shape[2], c=shape[3])
        self.off += size
        self.peak = max(self.peak, self.off)
        return a

    def alloc_top(self, shape, dtype):
        esz = 4 if dtype == F32 else 2
        n = 1
        for s in shape[1:]:
            n *= s
        size = (n * esz + 63) // 64 * 64
        save = self.off
        self.top -= size
        assert self.top >= self.off, ("SBUF arena overflow (top)", self.off, self.top)
        self.off = self.top
        lim = self.nbytes
        self.nbytes = self.top + size
        a = self.alloc(shape, dtype)
        self.nbytes = lim
        self.off = save
        return a

    def free_top(self):
        self.top = self.nbytes

    def mark(self):
        return self.off

    def release(self, m):
        self.off = m


class Rot:
    def __init__(self, arena, name, n, shape, dtype):
        self.bufs = [arena.alloc(shape, dtype) for _ in range(n)]
        self.name = name
        self.i = 0

    def next(self):
        j = self.i % len(self.bufs)
        self.i += 1
        return self.bufs[j], "%s%d" % (self.name, j)


def build_program(dbg=None):
    nc = bass.Bass("TRN2", target_bir_lowering=False)
    di = lambda n, s: nc.dram_tensor(n, list(s), F32, kind="ExternalInput").ap()
    x_d = di("x", (NB, SEQ, D))
    ctx_d = di("ctx", (NB, CTX, D))
    ccols_d = di("ccols", (128, 8, 3))
    wada_d = di("w_ada_t", (18, 128, 8, 512))
    bada_d = di("b_ada_cols", (128, 72))
    ng_d = di("norm_g_cols", (128, 6, 8))
    f_w13_d = [di("f1_w13", (22, 128, 2, 8, 128)), di("f2_w13", (22, 128, 2, 8, 128))]
    f_w2_d = [di("f1_w2", (11, 128, 2, D)), di("f2_w2", (11, 128, 2, D))]
    wqk_d = di("w_qk", (4, 128, 2, 8, 128))
    wva_d = di("w_va", (4, 128, 8, 128))
    wlora_d = di("w_lora", (128, 8, 1024))
    wkr_d = di("w_kr", (128, 8, 2, 96))
    wgate_d = di("w_gate", (8, 128, 2, 8, 128))
    bgate_d = di("b_gate_cols", (128, 2, 8))
    gq_d = di("g_q_cols", (128, 6))
    gkv_d = di("g_kv_cols", (128, 2))
    wuq_d = di("w_uq_t", (8, 128, 6, 2, 96))
    wkn_d = di("w_kn", (128, 2, 8, 64))
    wv_d = di("w_v", (128, 2, 8, 64))
    rope_d = di("rope_tab", (2, 32, NTOK))
    nab_d = di("na_bias", (8, 128, 12, 128))
    wo_d = di("w_o", (8, 128, 2, 4, 128))
    wout_d = di("w_out_t", (128, 8, D))
    ident_d = di("ident", (128, 128))
    permp_d = di("permp", (32, 96))
    out_d = nc.dram_tensor("out", [NB, SEQ, D], F32, kind="ExternalOutput").ap()
    x1s = nc.dram_tensor("x1s", [NTOK, D], F32, kind="Internal").ap()
    x2s = nc.dram_tensor("x2s", [SEQ, D], F32, kind="Internal").ap()
    dbg_out = {}
    if dbg:
        for n, s in dbg.items():
            dbg_out[n] = nc.dram_tensor("dbg_" + n, list(s), F32, kind="ExternalOutput").ap()

    P = Prog(nc)
    ARENA_BYTES = 200 * 1024
    with contextlib.ExitStack() as st:
        ar_t = st.enter_context(nc.sbuf_tensor("arena", [128, ARENA_BYTES // 4], F32))
        A = Arena(ar_t[:, :], ARENA_BYTES)
        banks = [st.enter_context(nc.psum_tensor("bank%d" % i, [128, 512], F32)) for i in range(8)]
        bankb = [b[:, :].bitcast(BF16) for b in banks]

        ident = A.alloc([128, 128], BF16)
        identf = A.alloc([128, 128], F32)
        onesf = A.alloc([128, 128], F32)
        mT = A.alloc([128, 72, 3], F32)
        ngc = A.alloc([128, 6, 8], F32)
        badac = A.alloc([128, 72], F32)
        ABc = A.alloc([128, 3, 3, 2, 8], F32)
        Cc = A.alloc([128, 3, 3, 8], F32)
        bgc = A.alloc([128, 2, 8], F32)
        gqc = A.alloc([128, 6], F32)
        gkvc = A.alloc([128, 2], F32)
        ss_rot = Rot(A, "ss", 6, [128, 4], F32)
        rs_rot = Rot(A, "rs", 6, [128, 2], F32)
        junk = A.alloc([128, 1024], BF16)

        dq = {"n": 0}

        def dma(eng, out, in_, reads=(), writes=(), chan=None):
            if chan is None:
                chan = "d%d" % dq["n"]
                dq["n"] += 1
            return P.add(eng, lambda e: e.dma_start(out=out, in_=in_), reads=reads, writes=writes, chan=chan)

        dma("sp", identf, ident_d, writes=["identf"], chan="g0")
        dma("pool", ident, ident_d, writes=["ident"], chan="g1")
        dma("sp", ngc, ng_d, writes=["ngc"], chan="g2")
        dma("sp", badac, bada_d, writes=["badac"], chan="g3")
        dma("sp", bgc, bgate_d, writes=["bgc"], chan="g4")
        dma("sp", gqc, gq_d, writes=["gqc"], chan="g5")
        dma("sp", gkvc, gkv_d, writes=["gkvc"], chan="g6")
        P.add("dve", lambda e: e.memset(onesf, 1.0), writes=["onesf"])
        neghalf = A.alloc([128, 2], F32)
        P.add("dve", lambda e: e.memset(neghalf, -0.5), writes=["neghalf"])

        cc = A.alloc([128, 8, 3], F32)
        scT = A.alloc([128, 8, 3], BF16)
        dma("sp", cc, ccols_d, writes=["cc"], chan="g7")
        P.add("act", lambda e: e.activation(out=scT, in_=cc, func=AF.Silu), reads=["cc"], writes=["scT"])

        scTf = A.alloc([128, 8, 3], F32)
        P.add("act", lambda e: e.activation(out=scTf, in_=cc, func=AF.Silu), reads=["cc"], writes=["scTf"])

        def ada_block(blk, wb, wres, bank, f32path=False):
            dma("sp" if f32path else "pool", wb, wada_d[blk], writes=[wres], chan="ld_" + wres)
            rhs_t = scTf if f32path else scT

            def mm(e):
                ins = None
                for f in range(4):
                    for k in range(8):
                        ins = e.matmul(banks[bank][:, f * 3:f * 3 + 3], lhsT=wb[:, k, f * 128:(f + 1) * 128],
                                       rhs=rhs_t[:, k, :], start=(k == 0), stop=(k == 7))
                return ins
            P.add("pe", mm, reads=[wres, "scT", "scTf"], writes=["bank%d" % bank])
            psv = banks[bank][:, 0:12].rearrange("p (a b) -> p a b", a=4)
            for s3 in range(3):
                P.add("dve", lambda e, s3=s3: e.tensor_tensor(out=mT[:, blk * 4:(blk + 1) * 4, s3], in0=psv[:, :, s3],
                                                              in1=badac[:, blk * 4:(blk + 1) * 4], op=ALU.add),
                      reads=["bank%d" % bank, "badac"], writes=["mT_%d" % blk])

        NORMS = ((0, 0, 1), (2, 3, 4), (4, 6, 7))
        CS = ((1, 2, 0.5), (3, 5, 1.0), (5, 8, 0.5))

        def derive(norms, cs):
            for s in range(3):
                for n_ in norms:
                    gi, sh, sc_ = NORMS[n_]
                    P.add("dve", lambda e, s=s, n_=n_, gi=gi, sc_=sc_: e.scalar_tensor_tensor(
                        out=ABc[:, s, n_, 0, :], in0=mT[:, sc_ * 8:(sc_ + 1) * 8, s], scalar=1.0, in1=ngc[:, gi, :],
                        op0=ALU.add, op1=ALU.mult), reads=["mT_%d" % (2 * sc_), "mT_%d" % (2 * sc_ + 1), "ngc"], writes=["ABc"])
                    P.add("dve", lambda e, s=s, n_=n_, sh=sh: e.tensor_copy(
                        out=ABc[:, s, n_, 1, :], in_=mT[:, sh * 8:(sh + 1) * 8, s]),
                        reads=["mT_%d" % (2 * sh), "mT_%d" % (2 * sh + 1)], writes=["ABc"])
                for w_ in cs:
                    gi, gt, wt = CS[w_]
                    P.add("dve", lambda e, s=s, w_=w_, gi=gi, gt=gt, wt=wt: e.scalar_tensor_tensor(
                        out=Cc[:, s, w_, :], in0=mT[:, gt * 8:(gt + 1) * 8, s], scalar=wt, in1=ngc[:, gi, :],
                        op0=ALU.mult, op1=ALU.mult), reads=["mT_%d" % (2 * gt), "mT_%d" % (2 * gt + 1), "ngc"], writes=["Cc"])

        mk = A.mark()
        wada = [A.alloc([128, 8, 512], BF16) for _ in range(2)]
        wadaf = [A.alloc([128, 8, 512], F32) for _ in range(2)]
        for blk in range(10 if ADABG else 18):
            if blk % 2 == 0:
                ada_block(blk, wada[(blk // 2) % 2], "wada%d" % ((blk // 2) % 2), blk % 4)
            else:
                ada_block(blk, wadaf[(blk // 2) % 2], "wadaf%d" % ((blk // 2) % 2), blk % 4, f32path=True)
        if ADABG:
            derive([0, 1], [0])
        else:
            derive([0, 1, 2], [0, 1, 2])
        P.barrier()
        A.release(mk)

        bcl = A.alloc([128, 8, 128], F32)

        def make_bc(dst, dst_res, col):
            for k in range(8):
                P.add("dve", lambda e, k=k: e.tensor_scalar(out=bcl[:, k, :], in0=onesf, scalar1=col[:, k:k + 1],
                                                            scalar2=None, op0=ALU.mult),
                      reads=["onesf", "Cc"], writes=["bcl%d" % k])
            for h in range(2):
                def mm(e, h=h):
                    ins = None
                    for kk in range(4):
                        k = h * 4 + kk
                        ins = e.matmul(banks[6 + h][:, kk * 128:(kk + 1) * 128], lhsT=bcl[:, k, :], rhs=identf,
                                       start=True, stop=True)
                    return ins
                P.add("pe", mm, reads=["bcl%d" % (h * 4 + kk) for kk in range(4)] + ["identf"], writes=["bank%d" % (6 + h)])
                P.add("act", lambda e, h=h: e.activation(out=dst[:, h * 512:(h + 1) * 512], in_=banks[6 + h][:, :], func=AF.Copy),
                      reads=["bank%d" % (6 + h)], writes=[dst_res])

        trb = {"i": 0, "banks": (2, 3)}

        def pn_a(xt, xt_res):
            ss, ssr = ss_rot.next()
            rs, rsr = rs_rot.next()
            P.add("act", lambda e: e.activation(out=junk, in_=xt, func=AF.Square, accum_out=ss[:, 0:1]),
                  reads=[xt_res], writes=[ssr, "junk"])
            P.add("dve", lambda e: e.tensor_scalar(out=rs[:, 0:1], in0=ss[:, 0:1], scalar1=1.0 / D, scalar2=EPS,
                                                   op0=ALU.mult, op1=ALU.add), reads=[ssr], writes=[rsr])
            P.add("pool", lambda e: e.tensor_tensor(out=rs[:, 1:2], in0=rs[:, 0:1], in1=neghalf[:, 0:1], op=ALU.pow),
                  reads=[rsr, "neghalf"], writes=[rsr + "p"])
            return (xt, xt_res, rs, rsr)

        def pn_c(st_, xn_rot):
            xt, xt_res, rs, rsr = st_
            xn, xnr = xn_rot.next()
            P.add("act", lambda e: e.activation(out=xn, in_=xt, func=AF.Copy, scale=rs[:, 1:2]),
                  reads=[xt_res, rsr + "p"], writes=[xnr])
            return (xn, xnr)

        def pn_t(st2, Acol, Bcol, dst_fn, dst_res):
            xn, xnr = st2
            bi = trb["banks"][trb["i"] % len(trb["banks"])]
            trb["i"] += 1
            pb = bankb[bi]

            def tr(e):
                ins = None
                for k in range(8):
                    ins = e.transpose(out=pb[:, k * 128:(k + 1) * 128], in_=xn[:, k * 128:(k + 1) * 128], identity=ident)
                return ins
            P.add("pe", tr, reads=[xnr, "ident"], writes=["bank%d" % bi])
            for k in range(8):
                P.add("dve", lambda e, k=k: e.tensor_scalar(out=dst_fn(k), in0=pb[:, k * 128:(k + 1) * 128],
                                                            scalar1=Acol[:, k:k + 1], scalar2=Bcol[:, k:k + 1],
                                                            op0=ALU.mult, op1=ALU.add),
                      reads=["bank%d" % bi, "ABc"], writes=[dst_res])

        def pn_steps(items, xrot, xn_rot):
            state = {}
            steps = []

            state2 = {}

            def mk(j):
                def step():
                    if j < len(items):
                        src, sres, Ac, Bc, dfn, dres = items[j]
                        xt, xr = xrot.next()
                        dma("sp", xt, src, reads=[sres], writes=[xr], chan="ld_" + xr)
                        state[j] = pn_a(xt, xr)
                    if 1 <= j <= len(items):
                        state2[j - 1] = pn_c(state.pop(j - 1), xn_rot)
                    if 2 <= j:
                        src, sres, Ac, Bc, dfn, dres = items[j - 2]
                        pn_t(state2.pop(j - 2), Ac, Bc, dfn, dres)
                return step
            for j in range(len(items) + 2):
                steps.append(mk(j))
            return steps

        def epilogue(y_banks, xt, xt_res, Cbc, Cres, ytmp_rot, nfeat=D):
            ss, ssr = ss_rot.next()
            rs, rsr = rs_rot.next()
            yt, ytr = ytmp_rot.next()
            for h in range(2):
                P.add("act", lambda e, h=h: e.activation(out=junk[:, h * 512:(h + 1) * 512], in_=banks[y_banks[h]][:, :],
                                                         func=AF.Square, accum_out=ss[:, h:h + 1]),
                      reads=["bank%d" % y_banks[h]], writes=[ssr + "_%d" % h, "junk"])
            P.add("dve", lambda e: e.tensor_scalar(out=rs[:, 0:1], in0=ss[:, 0:1], scalar1=ss[:, 1:2], scalar2=1.0 / nfeat,
                                                   op0=ALU.add, op1=ALU.mult), reads=[ssr + "_0", ssr + "_1"], writes=[rsr])
            P.add("dve", lambda e: e.tensor_scalar(out=rs[:, 0:1], in0=rs[:, 0:1], scalar1=EPS, scalar2=None,
                                                   op0=ALU.add), reads=[rsr], writes=[rsr])
            P.add("pool", lambda e: e.tensor_tensor(out=rs[:, 1:2], in0=rs[:, 0:1], in1=neghalf[:, 0:1], op=ALU.pow),
                  reads=[rsr, "neghalf"], writes=[rsr + "p"])
            for h in range(2):
                P.add("dve", lambda e, h=h: e.scalar_tensor_tensor(
                    out=yt[:, h * 512:(h + 1) * 512], in0=banks[y_banks[h]][:, :], scalar=rs[:, 1:2],
                    in1=Cbc[:, h * 512:(h + 1) * 512], op0=ALU.mult, op1=ALU.mult),
                    reads=["bank%d" % y_banks[h], rsr + "p", Cres], writes=[ytr + "_%d" % h])
            P.add("pool", lambda e: e.tensor_tensor(out=xt, in0=xt, in1=yt, op=ALU.add),
                  reads=[xt_res, ytr + "_0", ytr + "_1"], writes=[xt_res])

        def ffn_phase(fi, groups, normidx, cidx):
            mk = A.mark()
            w2 = A.alloc([128, 22, D], BF16)
            NW = 4
            w13 = [A.alloc([128, 2, 8, 128], BF16) for _ in range(NW)]
            TMAX = max(len(g[1]) for g in groups) * 128
            aT = A.alloc([128, 22, TMAX], BF16)
            hTs = [A.alloc([128, 8, TMAX], BF16) for _ in range(2)]
            x_rot = Rot(A, "fx", 3, [128, D], F32)
            xd_rot = Rot(A, "fxd", 3, [128, D], F32)
            xn_rot = Rot(A, "fxn", 3, [128, D], BF16)
            yt_rot = Rot(A, "fyt", 2, [128, D], F32)
            sl_rot = Rot(A, "fsl", 2, [128, 512], F32)
            streams = sorted(set(tl[4] for g in groups for tl in g[1]))
            Cbc = {}
            for s in streams:
                Cbc[s] = A.alloc([128, D], F32)
                make_bc(Cbc[s], "Cbc%d" % s, Cc[:, s, cidx, :])
            ub = {"i": 0}
            wst = {"issued": 0, "w2": 0}
            NCH = 22 * len(groups)

            def w_prefetch(upto):
                while wst["issued"] < min(upto, NCH):
                    gc = wst["issued"]
                    wr = "w13_%d" % (gc % NW)
                    dma("pool", w13[gc % NW], f_w13_d[fi][gc % 22], writes=[wr], chan="ld_" + wr)
                    wst["issued"] += 1
                    if gc >= 3 and gc % 2 == 1 and wst["w2"] < 11:
                        blk = wst["w2"]
                        dma("pool", w2[:, blk * 2:blk * 2 + 2, :], f_w2_d[fi][blk], writes=["w2_%d" % blk], chan="w2ld")
                        wst["w2"] += 1
            w_prefetch(NW)

            def group_pn_steps(gi):
                _, tiles = groups[gi]
                hT = hTs[gi % 2]
                items = []
                for tt, (src, dst, sres, dres, s) in enumerate(tiles):
                    items.append((src, sres, ABc[:, s, normidx, 0, :], ABc[:, s, normidx, 1, :],
                                  (lambda k, tt=tt, hT=hT: hT[:, k, tt * 128:(tt + 1) * 128]), "hT%d_%d" % (gi % 2, tt)))
                return pn_steps(items, x_rot, xn_rot)

            for st_ in group_pn_steps(0):
                st_()
            for gi, (_, tiles) in enumerate(groups):
                hT = hTs[gi % 2]
                hp = gi % 2
                nt = len(tiles)
                T = nt * 128
                ncg = (T + 511) // 512
                CW = T // ncg
                assert CW % 128 == 0 and CW * ncg == T
                for c in range(22):
                    gc = gi * 22 + c
                    w_prefetch(gc + NW)
                    wb = w13[gc % NW]
                    wr = "w13_%d" % (gc % NW)
                    j = 0
                    if True:
                        for cg in range(ncg):
                            b1 = ub["i"] % 2
                            b3 = 2 + ub["i"] % 2
                            ub["i"] += 1
                            hres = ["hT%d_%d" % (hp, t) for t in range(cg * CW // 128, (cg + 1) * CW // 128)]

                            def mm(e, wb=wb, j=j, cg=cg, b1=b1, b3=b3, CW=CW, hT=hT):
                                ins = None
                                for wi, bb in ((0, b1), (1, b3)):
                                    for k in range(8):
                                        ins = e.matmul(banks[bb][:, 0:CW], lhsT=wb[:, wi, k, :],
                                                       rhs=hT[:, k, cg * CW:(cg + 1) * CW], start=(k == 0), stop=(k == 7))
                                return ins
                            P.add("pe", mm, reads=[wr] + hres, writes=["bank%d" % b1, "bank%d" % b3])
                            sl, slr = sl_rot.next()
                            P.add("act", lambda e, sl=sl, b1=b1, CW=CW: e.activation(out=sl[:, 0:CW], in_=banks[b1][:, 0:CW], func=AF.Silu),
                                  reads=["bank%d" % b1], writes=[slr])
                            P.add("dve", lambda e, sl=sl, b3=b3, c=c, cg=cg, CW=CW: e.tensor_tensor(
                                out=aT[:, c, cg * CW:(cg + 1) * CW], in0=sl[:, 0:CW], in1=banks[b3][:, 0:CW], op=ALU.mult),
                                reads=[slr, "bank%d" % b3], writes=["aT_%d_%d" % (c, cg)])
                nxt = group_pn_steps(gi + 1) if gi + 1 < len(groups) else []
                per = (len(nxt) + nt - 1) // nt if nxt else 0
                ni = 0
                pend = None
                for tt, (src, dst, sres, dres, s) in enumerate(tiles):
                    yb = (4, 5) if tt % 2 == 0 else (6, 7)
                    cg = tt * 128 // CW
                    xt, xr = xd_rot.next()
                    dma("sp", xt, src, reads=[sres], writes=[xr], chan="ld_" + xr)
                    if pend is not None:
                        dma("pool", pend[0], pend[1], reads=[pend[2]], writes=[pend[3]], chan="st_" + pend[2])
                        pend = None
                    for h in range(2):
                        def mm(e, tt=tt, h=h, yb=yb):
                            ins = None
                            for c in range(22):
                                ins = e.matmul(banks[yb[h]][:, :], lhsT=aT[:, c, tt * 128:(tt + 1) * 128],
                                               rhs=w2[:, c, h * 512:(h + 1) * 512], start=(c == 0), stop=(c == 21))
                            return ins
                        P.add("pe", mm, reads=["aT_%d_%d" % (c, cg) for c in range(22)] + ["w2_%d" % b for b in range(11)],
                              writes=["bank%d" % yb[h]])
                    for _ in range(per):
                        if ni < len(nxt):
                            nxt[ni]()
                            ni += 1
                    epilogue(yb, xt, xr, Cbc[s], "Cbc%d" % s, yt_rot)
                    dma("pool", dst, xt, reads=[xr], writes=[dres], chan="st_" + xr)
                while ni < len(nxt):
                    nxt[ni]()
                    ni += 1
            P.barrier()
            A.release(mk)

        for b in range(NB):
            alltiles = [(ctx_d[b, t * 128:(t + 1) * 128, :], x1s[t * 128:(t + 1) * 128, :], "xin", "x1s_%d" % t, 2) for t in range(2)]
            alltiles += [(x_d[b, t * 128:(t + 1) * 128, :], x1s[256 + t * 128:256 + (t + 1) * 128, :], "xin", "x1s_%d" % (2 + t), b)
                         for t in range(16)]
            ffn_phase(0, [(None, alltiles[g * 6:(g + 1) * 6]) for g in range(3)], 0, 0)

            mkb = A.mark()
            oT = A.alloc([128, 8, SEQ], BF16)
            hT2 = A.alloc_top([128, 8, NTOK], BF16)
            mk2 = A.mark()
            p2_items = []
            for t in range(18):
                s = 2 if t < 2 else b
                p2_items.append((x1s[t * 128:(t + 1) * 128, :], "x1s_%d" % t, ABc[:, s, 1, 0, :], ABc[:, s, 1, 1, :],
                                 (lambda k, t=t: hT2[:, k, t * 128:(t + 1) * 128]), "hT2_%d" % t))

            wqk = [A.alloc([128, 2, 8, 128], BF16) for _ in range(2)]
            wva = [A.alloc([128, 8, 128], BF16) for _ in range(2)]
            qaT = [A.alloc([128, SEQ], BF16) for _ in range(2)]
            kaT = [A.alloc([128, NTOK], BF16) for _ in range(2)]
            vaug = [A.alloc([128, 18, 2, 128], BF16) for _ in range(2)]
            nab = [A.alloc([128, 12, 128], F32) for _ in range(2)]
            rec_rot = Rot(A, "nrec", 2, [128, 512], F32)
            px_rot = Rot(A, "px", 4, [128, D], F32)
            pxn_rot = Rot(A, "pxn", 3, [128, D], BF16)
            for i2 in range(2):
                P.add("pool", lambda e, i2=i2: e.memset(vaug[i2], 1.0), writes=["vaug%d_%d" % (i2, t) for t in range(18)])
            ubn = {"i": 0}

            def na_proj_pieces(pr):
                pb_ = pr % 2
                pieces = []

                def p_load():
                    dma("pool", wqk[pb_], wqk_d[pr], writes=["wqk%d" % pb_], chan="ld_wqk%d" % pb_)
                    dma("pool", wva[pb_], wva_d[pr], writes=["wva%d" % pb_], chan="ld_wva%d" % pb_)
                pieces.append(p_load)
                for which, dstT, ncols, t0 in ((0, qaT[pb_], SEQ, 2), (1, kaT[pb_], NTOK, 0)):
                    c0 = 0
                    while c0 < ncols:
                        cw = 256 if (which == 1 and c0 == 0) else 512

                        def p_qk(which=which, dstT=dstT, c0=c0, cw=cw, t0=t0):
                            bb = 6 + ubn["i"] % 2
                            ubn["i"] += 1
                            tl = [t0 + (c0 + q) // 128 for q in range(0, cw, 128)]
                            src0 = c0 + (256 if which == 0 else 0)

                            def mm(e):
                                ins = None
                                for k in range(8):
                                    ins = e.matmul(banks[bb][:, 0:cw], lhsT=wqk[pb_][:, which, k, :], rhs=hT2[:, k, src0:src0 + cw],
                                                   start=(k == 0), stop=(k == 7))
                                return ins
                            P.add("pe", mm, reads=["wqk%d" % pb_] + ["hT2_%d" % t for t in tl], writes=["bank%d" % bb])
                            rname = ("qaT%d_%d" if which == 0 else "kaT%d_%d") % (pb_, c0 // 512 if which == 0 else (0 if c0 == 0 else 1 + (c0 - 256) // 512))
                            P.add("dve", lambda e: e.tensor_copy(out=dstT[:, c0:c0 + cw], in_=banks[bb][:, 0:cw]),
                                  reads=["bank%d" % bb], writes=[rname])
                        pieces.append(p_qk)
                        c0 += cw
                for t4 in range(0, 18, 4):
                    def p_v(t4=t4):
                        tn = min(4, 18 - t4)
                        bb = 6 + ubn["i"] % 2
                        ubn["i"] += 1

                        def mm(e):
                            ins = None
                            for q in range(tn):
                                for k in range(8):
                                    ins = e.matmul(banks[bb][:, q * 128:(q + 1) * 128], lhsT=hT2[:, k, (t4 + q) * 128:(t4 + q + 1) * 128],
                                                   rhs=wva[pb_][:, k, :], start=(k == 0), stop=(k == 7))
                            return ins
                        P.add("pe", mm, reads=["wva%d" % pb_] + ["hT2_%d" % (t4 + q) for q in range(tn)], writes=["bank%d" % bb])
                        for e_ in range(2):
                            P.add("act", lambda e, e_=e_: e.activation(
                                out=vaug[pb_][:, t4:t4 + tn, e_, e_ * 64:(e_ + 1) * 64],
                                in_=banks[bb][:, 0:tn * 128].rearrange("p (a b) -> p a b", a=tn)[:, :, e_ * 64:(e_ + 1) * 64], func=AF.Copy),
                                reads=["bank%d" % bb], writes=["vaug%d_%d" % (pb_, t4 + q) for q in range(tn)])
                    pieces.append(p_v)
                return pieces

            def na_proj(pr):
                for pc_ in na_proj_pieces(pr):
                    pc_()

            nab8 = [A.alloc([128, 12, 128], BF16) for _ in range(2)]

            def na_bias_load(h):
                nbr = "nab%d" % (h % 2)
                dma("sp", nab[h % 2], nab_d[h], writes=[nbr], chan="ld_" + nbr)
                P.add("dve", lambda e: e.tensor_scalar(out=nab8[h % 2], in0=nab[h % 2], scalar1=8.0, scalar2=None, op0=ALU.mult),
                      reads=[nbr], writes=["nab8_%d" % (h % 2)])

            def js_of(i):
                if i <= 1:
                    return list(range(0, 4))
                if i >= 14:
                    return list(range(12, 16))
                return list(range(i - 2, i + 3))
            IR = []
            for j in range(16):
                ii = [i for i in range(16) if j in js_of(i)]
                IR.append((ii[0], ii[-1]))

            def tile_idx(j, i):
                return (j - i + 3) if i in (0, 1, 14, 15) else (9 - j + i)
            ptl_rot = Rot(A, "nptl", 4, [128, 768], BF16)
            ptc_rot = Rot(A, "nptc", 2, [128, 2, 512], BF16)
            cunits = [(pr, e_, j) for pr in range(4) for e_ in range(2) for j in range(16)]
            cst = {}
            gcount = {"i": 0}
            gbank = {}

            def na_cS(ui):
                pr, e_, j = cunits[ui]
                pb_ = pr % 2
                h = pr * 2 + e_
                base = e_ * 64
                nb_ = nab[h % 2]
                nbr = "nab%d" % (h % 2)
                i0_, i1_ = IR[j]
                L = i1_ - i0_ + 1
                LA = min(L, 4)
                bset = ui % 2
                bA, bB = 2 * bset, 2 * bset + 1
                kq = kaT[pb_]
                qq = qaT[pb_]

                nb8 = nab8[h % 2]
                pt, ptr = ptl_rot.next()
                runs = []
                for i in range(i0_, i1_ + 1):
                    col = i - i0_
                    ti = tile_idx(j, i)
                    bk = bA if col < 4 else bB
                    if runs and runs[-1][3] == bk and runs[-1][1] + runs[-1][2] == ti and runs[-1][0] + runs[-1][2] == col:
                        runs[-1][2] += 1
                    else:
                        runs.append([col, ti, 1, bk])

                def mmS(e):
                    lhsT = kq[base:base + 64, 256 + j * 128:256 + (j + 1) * 128]
                    ins = e.matmul(banks[bA][:, 0:LA * 128], lhsT=lhsT, rhs=qq[base:base + 64, i0_ * 128:(i0_ + LA) * 128], start=True, stop=False)
                    if L > 4:
                        ins = e.matmul(banks[bB][:, 0:(L - 4) * 128], lhsT=lhsT, rhs=qq[base:base + 64, (i0_ + 4) * 128:(i1_ + 1) * 128], start=True, stop=False)
                    lastA = max(rn for rn, r_ in enumerate(runs) if r_[3] == bA)
                    lastB = max([rn for rn, r_ in enumerate(runs) if r_[3] == bB] or [-1])
                    for rn, (col, ti, n, bk) in enumerate(runs):
                        pc = col if bk == bA else col - 4
                        ins = e.matmul(banks[bk][:, pc * 128:(pc + n) * 128], lhsT=ident, rhs=nb8[:, ti:ti + n, :].rearrange("p a b -> p (a b)"),
                                       start=False, stop=(rn == lastA or rn == lastB))
                    return ins
                qres = sorted(set("qaT%d_%d" % (pb_, i // 4) for i in range(i0_, i1_ + 1)))
                P.add("pe", mmS, reads=["kaT%d_%d" % (pb_, 1 + j // 4), "nab8_%d" % (h % 2), "ident"] + qres, writes=["bank%d" % bA, "bank%d" % bB])
                P.add("act", lambda e: e.activation(out=pt[:, 0:LA * 128], in_=banks[bA][:, 0:LA * 128], func=AF.Exp, scale=0.125),
                      reads=["bank%d" % bA], writes=[ptr + "a"])
                pres = [ptr + "a"]
                if L > 4:
                    P.add("act", lambda e: e.activation(out=pt[:, 512:L * 128], in_=banks[bB][:, 0:(L - 4) * 128], func=AF.Exp, scale=0.125),
                          reads=["bank%d" % bB], writes=[ptr + "b"])
                    pres.append(ptr + "b")
                cst[ui] = (pt, pres)

            def na_ctx(pr, e_, g):
                pb_ = pr % 2
                base = e_ * 64
                kq = kaT[pb_]
                qq = qaT[pb_]
                ptc, ptcr = ptc_rot.next()
                for n in range(2):
                    bb = 6 + n
                    P.add("pe", lambda e, n=n, bb=bb: e.matmul(banks[bb][:, :], lhsT=kq[base:base + 64, n * 128:(n + 1) * 128],
                                                           rhs=qq[base:base + 64, g * 512:(g + 1) * 512], start=True, stop=True),
                          reads=["kaT%d_0" % pb_, "qaT%d_%d" % (pb_, g)], writes=["bank%d" % bb])
                    P.add("act", lambda e, n=n, bb=bb: e.activation(out=ptc[:, n, :], in_=banks[bb][:, :], func=AF.Exp, scale=0.125),
                          reads=["bank%d" % bb], writes=[ptcr + "_%d" % n])
                bO = 4 + gcount["i"] % 2
                gcount["i"] += 1
                gbank[(pr, e_, g)] = bO

                def mmO(e):
                    ins = None
                    for n in range(2):
                        ins = e.matmul(banks[bO][:, :], lhsT=vaug[pb_][:, n, e_, :], rhs=ptc[:, n, :], start=(n == 0), stop=False,
                                       skip_group_check=True)
                    return ins
                P.add("pe", mmO, reads=[ptcr + "_0", ptcr + "_1", "vaug%d_0" % pb_, "vaug%d_1" % pb_], writes=["bank%d" % bO])

            def na_cO(ui):
                pr, e_, j = cunits[ui]
                pb_ = pr % 2
                pt, ptr = cst.pop(ui)
                i0_, i1_ = IR[j]
                for g in range(i0_ // 4, i1_ // 4 + 1):
                    if max(0, 4 * g - 2) == j:
                        na_ctx(pr, e_, g)
                    ia = max(i0_, 4 * g)
                    ib = min(i1_, 4 * g + 3)
                    bO = gbank[(pr, e_, g)]
                    P.add("pe", lambda e, ia=ia, ib=ib, bO=bO, g=g: e.matmul(
                        banks[bO][:, (ia - 4 * g) * 128:(ib - 4 * g + 1) * 128], lhsT=vaug[pb_][:, 2 + j, e_, :],
                        rhs=pt[:, (ia - i0_) * 128:(ib - i0_ + 1) * 128], start=False, stop=False, skip_group_check=True),
                        reads=ptr + ["vaug%d_%d" % (pb_, 2 + j)], writes=["bank%d" % bO])
                    if j == min(15, 4 * g + 5):
                        orow = slice(0, 64) if e_ == 0 else slice(64, 128)
                        srow = slice(64, 128) if e_ == 0 else slice(0, 64)
                        rec, recr = rec_rot.next()
                        P.add("dve", lambda e, rec=rec, bO=bO, orow=orow, srow=srow: e.reciprocal(out=rec[orow, :], in_=banks[bO][srow, :]),
                              reads=["bank%d" % bO], writes=[recr])
                        P.add("dve", lambda e, rec=rec, bO=bO, orow=orow, g=g: e.tensor_tensor(
                            out=oT[orow, pr, g * 512:(g + 1) * 512], in0=banks[bO][orow, :], in1=rec[orow, :], op=ALU.mult),
                            reads=["bank%d" % bO, recr], writes=["oT_%d_%d_%d" % (pr, g, e_)])

            SKEW = 2
            if b == 0 and ADABG:
                wada2 = [A.alloc([128, 8, 512], BF16) for _ in range(2)]
            p2_steps = pn_steps(p2_items, px_rot, pxn_rot)
            pcs0 = na_proj_pieces(0)
            need0 = [-1, 5, 9, 13, 17, 1, 5, 9, 13, 17, 3, 7, 11, 15, 17]
            assert len(pcs0) == len(need0)
            pcs0[0]()
            order0 = sorted(range(1, len(pcs0)), key=lambda q: (need0[q], q))
            oi = 0
            for si, st_ in enumerate(p2_steps):
                st_()
                while oi < len(order0) and need0[order0[oi]] <= si - 2:
                    pcs0[order0[oi]]()
                    oi += 1
            while oi < len(order0):
                pcs0[order0[oi]]()
                oi += 1
            na_bias_load(0)
            pend_np = []
            for ui in range(len(cunits) + SKEW):
                if ui < len(cunits):
                    pr, e_, j = cunits[ui]
                    if j == 8 and pr * 2 + e_ + 1 < 8:
                        na_bias_load(pr * 2 + e_ + 1)
                    if e_ == 0 and j == 4 and pr + 1 < 4:
                        pend_np = na_proj_pieces(pr + 1)
                    if e_ == 1 and j == 15:
                        for pc_ in pend_np:
                            pc_()
                        pend_np = []
                    elif pend_np:
                        pend_np.pop(0)()
                    na_cS(ui)
                if ui >= SKEW:
                    na_cO(ui - SKEW)
            P.barrier()
            A.release(mk2)

            cqnT = A.alloc([128, 6, SEQ], BF16)
            ckvnT = A.alloc([128, 2, NTOK], BF16)
            KRT = A.alloc([128, NTOK], BF16)
            rope = A.alloc([128, 2, NTOK], F32)
            mk4 = A.mark()
            wlora = A.alloc([128, 8, 1024], BF16)
            wkr = A.alloc([128, 8, 2, 96], BF16)
            cn_rot = Rot(A, "lcn", 4, [128, 1024], BF16)
            rt_rot = Rot(A, "lrt", 2, [128, 2, 512], F32)
            dma("pool", wlora, wlora_d, writes=["wlora0", "wlora1"], chan="ld_wlora")
            dma("pool", wkr, wkr_d, writes=["wkr"], chan="ld_wkr")
            for cs in range(2):
                dma("sp", rope[64:96, cs, :], rope_d[cs], writes=["rope%d" % cs], chan="ld_rope%d" % cs)
            lst = {}

            def lora_A(t):
                lat = t >= 2
                yb = ((0, 1), (4, 5), (6, 7))[t % 3] if LORA3 else ((4, 5) if t % 2 == 0 else (6, 7))

                def mm(e):
                    ins = None
                    for hh in range(2):
                        if hh == 0 and not lat:
                            continue
                        for k in range(8):
                            ins = e.matmul(banks[yb[hh]][:, :], lhsT=hT2[:, k, t * 128:(t + 1) * 128], rhs=wlora[:, k, hh * 512:(hh + 1) * 512],
                                           start=(k == 0), stop=(k == 7))
                    return ins
                P.add("pe", mm, reads=["hT2_%d" % t, "wlora0", "wlora1"], writes=["bank%d" % yb[0], "bank%d" % yb[1]])
                ss, ssr = ss_rot.next()
                rs, rsr = rs_rot.next()
                if lat:
                    P.add("act", lambda e: e.activation(out=junk[:, 0:512], in_=banks[yb[0]][:, :], func=AF.Square, accum_out=ss[:, 0:1]),
                          reads=["bank%d" % yb[0]], writes=[ssr + "_0", "junk"])
                    P.add("act", lambda e: e.activation(out=junk[:, 512:768], in_=banks[yb[1]][:, 0:256], func=AF.Square, accum_out=ss[:, 1:2]),
                          reads=["bank%d" % yb[1]], writes=[ssr + "_1", "junk"])
                P.add("act", lambda e: e.activation(out=junk[:, 768:1024], in_=banks[yb[1]][:, 256:512], func=AF.Square, accum_out=ss[:, 2:3]),
                      reads=["bank%d" % yb[1]], writes=[ssr + "_2", "junk"])
                if lat:
                    P.add("dve", lambda e: e.tensor_scalar(out=rs[:, 0:1], in0=ss[:, 0:1], scalar1=ss[:, 1:2], scalar2=1.0 / 768,
                                                           op0=ALU.add, op1=ALU.mult), reads=[ssr + "_0", ssr + "_1"], writes=[rsr + "q"])
                    P.add("dve", lambda e: e.tensor_scalar(out=rs[:, 0:1], in0=rs[:, 0:1], scalar1=EPS, scalar2=None,
                                                           op0=ALU.add), reads=[rsr + "q"], writes=[rsr + "q"])
                P.add("dve", lambda e: e.tensor_scalar(out=rs[:, 1:2], in0=ss[:, 2:3], scalar1=1.0 / 256, scalar2=EPS,
                                                       op0=ALU.mult, op1=ALU.add), reads=[ssr + "_2"], writes=[rsr + "k"])
                c0_ = 0 if lat else 1
                P.add("pool", lambda e: e.tensor_tensor(out=ss[:, c0_:2], in0=rs[:, c0_:2], in1=neghalf[:, c0_:2], op=ALU.pow),
                      reads=([rsr + "q"] if lat else []) + [rsr + "k", "neghalf", ssr + "_0", ssr + "_1"], writes=[ssr + "p"])
                lst[t] = (yb, ss, ssr, lat)

            lst2 = {}

            def lora_Bc(t):
                yb, ss, ssr, lat = lst.pop(t)
                cn, cnr = cn_rot.next()
                lst2[t] = (cn, cnr, lat)
                if lat:
                    P.add("act", lambda e: e.activation(out=cn[:, 0:512], in_=banks[yb[0]][:, :], func=AF.Copy, scale=ss[:, 0:1]),
                          reads=["bank%d" % yb[0], ssr + "p"], writes=[cnr + "_0"])
                    P.add("act", lambda e: e.activation(out=cn[:, 512:768], in_=banks[yb[1]][:, 0:256], func=AF.Copy, scale=ss[:, 0:1]),
                          reads=["bank%d" % yb[1], ssr + "p"], writes=[cnr + "_1"])
                P.add("act", lambda e: e.activation(out=cn[:, 768:1024], in_=banks[yb[1]][:, 256:512], func=AF.Copy, scale=ss[:, 1:2]),
                      reads=["bank%d" % yb[1], ssr + "p"], writes=[cnr + "_2"])

            def lora_Bt(t):
                cn, cnr, lat = lst2.pop(t)
                bi = 2 + (t % 2)
                pb = bankb[bi]
                k0 = 0 if lat else 6

                def tr(e):
                    ins = None
                    for k in range(k0, 8):
                        ins = e.transpose(out=pb[:, k * 128:(k + 1) * 128], in_=cn[:, k * 128:(k + 1) * 128], identity=ident)
                    return ins
                P.add("pe", tr, reads=([cnr + "_0", cnr + "_1"] if lat else []) + [cnr + "_2", "ident"], writes=["bank%d" % bi])
                if lat:
                    for k in range(6):
                        P.add("dve", lambda e, k=k: e.tensor_scalar(
                            out=cqnT[:, k, (t - 2) * 128:(t - 1) * 128], in0=pb[:, k * 128:(k + 1) * 128], scalar1=gqc[:, k:k + 1], scalar2=None,
                            op0=ALU.mult), reads=["bank%d" % bi, "gqc"], writes=["cqnT_%d" % (t - 2)])
                for k in range(2):
                    P.add("dve", lambda e, k=k: e.tensor_scalar(
                        out=ckvnT[:, k, t * 128:(t + 1) * 128], in0=pb[:, (6 + k) * 128:(7 + k) * 128], scalar1=gkvc[:, k:k + 1], scalar2=None,
                        op0=ALU.mult), reads=["bank%d" % bi, "gkvc"], writes=["ckvnT_%d" % t])

            for t in range(20):
                if t < 18:
                    lora_A(t)
                if 1 <= t <= 18:
                    lora_Bc(t - 1)
                if t >= 2:
                    lora_Bt(t - 2)
            c0 = 0
            ui = 0
            while c0 < NTOK:
                cw = 256 if c0 == 0 else 512
                bA, bB = ui % 2, 2 + ui % 2
                ui += 1
                tl = [(c0 + q) // 128 for q in range(0, cw, 128)]

                def mm(e, bA=bA, bB=bB, c0=c0, cw=cw):
                    ins = None
                    for wi, bb in ((0, bA), (1, bB)):
                        for k in range(8):
                            ins = e.matmul(banks[bb][0:96, 0:cw], lhsT=wkr[:, k, wi, :], rhs=hT2[:, k, c0:c0 + cw], start=(k == 0), stop=(k == 7))
                    return ins
                P.add("pe", mm, reads=["wkr"] + ["hT2_%d" % t for t in tl], writes=["bank%d" % bA, "bank%d" % bB])
                rt, rtr = rt_rot.next()
                P.add("dve", lambda e, rt=rt, bA=bA, c0=c0, cw=cw: e.tensor_tensor(out=rt[64:96, 0, 0:cw], in0=banks[bA][64:96, 0:cw], in1=rope[64:96, 0, c0:c0 + cw], op=ALU.mult),
                      reads=["bank%d" % bA, "rope0"], writes=[rtr + "a"])
                P.add("dve", lambda e, rt=rt, bB=bB, c0=c0, cw=cw: e.tensor_tensor(out=rt[64:96, 1, 0:cw], in0=banks[bB][64:96, 0:cw], in1=rope[64:96, 1, c0:c0 + cw], op=ALU.mult),
                      reads=["bank%d" % bB, "rope1"], writes=[rtr + "b"])
                P.add("pool", lambda e, rt=rt, c0=c0, cw=cw: e.tensor_tensor(out=KRT[64:96, c0:c0 + cw], in0=rt[64:96, 0, 0:cw], in1=rt[64:96, 1, 0:cw], op=ALU.add),
                      reads=[rtr + "a", rtr + "b"], writes=["KRT"])
                c0 += cw
            P.barrier()
            A.release(mk4)
            A.free_top()

            wkn = A.alloc([128, 2, 8, 64], BF16)
            wv = A.alloc([128, 2, 8, 64], BF16)
            wuq = [A.alloc([128, 6, 2, 96], BF16) for _ in range(2)]
            QT = [A.alloc([128, SEQ], BF16) for _ in range(2)]
            KT = [A.alloc([128, NTOK], BF16) for _ in range(2)]
            vaugm = [A.alloc([128, 18, 128], BF16) for _ in range(2)]
            PT = [A.alloc([128, 18, 512], BF16) for _ in range(2)]
            rt_rot = Rot(A, "mrt", 2, [128, 2, 512], F32)
            rec_rot = Rot(A, "mrec", 2, [128, 512], F32)
            raw_rot = Rot(A, "mraw", 2, [128, 512], BF16)
            permp = A.alloc([128, 96], BF16)
            dma("pool", permp[64:96, :], permp_d, writes=["permp"], chan="ld_permp")
            dma("pool", wkn, wkn_d, writes=["wkn"], chan="ld_wkn")
            dma("pool", wv, wv_d, writes=["wv"], chan="ld_wv")
            for i2 in range(2):
                P.add("pool", lambda e, i2=i2: e.memset(vaugm[i2], 1.0), writes=["vaugm%d" % i2])
            SC = float(96 ** -0.5)
            uim = {"i": 0}

            def mla_proj_pieces(h):
                hb = h % 2
                pieces = []

                def p_load():
                    dma("pool", wuq[hb], wuq_d[h], writes=["wuq%d" % hb], chan="ld_wuq%d" % hb)
                pieces.append(p_load)
                for cg in range(4):
                    bA, bB = 6, 7
                    st_ = {}

                    def p_a(cg=cg, st_=st_):
                        def mm(e):
                            ins = None
                            for k in range(6):
                                ins = e.matmul(banks[bA][0:96, :], lhsT=wuq[hb][:, k, 0, :], rhs=cqnT[:, k, cg * 512:(cg + 1) * 512], start=(k == 0), stop=(k == 5))
                            return ins
                        P.add("pe", mm, reads=["wuq%d" % hb] + ["cqnT_%d" % t for t in range(cg * 4, cg * 4 + 4)], writes=["bank%d" % bA])
                        qr = "QT%d_%d" % (hb, cg)
                        P.add("dve", lambda e: e.tensor_copy(out=QT[hb][0:64, cg * 512:(cg + 1) * 512], in_=banks[bA][0:64, :]),
                              reads=["bank%d" % bA], writes=[qr + "n"])
                        rw, rwr = raw_rot.next()
                        P.add("dve", lambda e: e.tensor_copy(out=rw[64:96, :], in_=banks[bA][64:96, :]), reads=["bank%d" % bA], writes=[rwr])
                        rt, rtr = rt_rot.next()
                        c0 = 256 + cg * 512
                        P.add("dve", lambda e: e.tensor_tensor(out=rt[64:96, 0, :], in0=banks[bA][64:96, :], in1=rope[64:96, 0, c0:c0 + 512], op=ALU.mult),
                              reads=["bank%d" % bA, "rope0"], writes=[rtr + "a"])
                        st_["v"] = (rw, rwr, rt, rtr, c0, qr)

                    def p_b(cg=cg, st_=st_):
                        rw, rwr, rt, rtr, c0, qr = st_["v"]
                        P.add("pe", lambda e: e.matmul(banks[bB][0:96, :], lhsT=permp[64:96, :], rhs=rw[64:96, :], start=True, stop=True),
                              reads=[rwr, "permp"], writes=["bank%d" % bB])
                        P.add("dve", lambda e: e.tensor_tensor(out=rt[64:96, 1, :], in0=banks[bB][64:96, :], in1=rope[64:96, 1, c0:c0 + 512], op=ALU.mult),
                              reads=["bank%d" % bB, "rope1"], writes=[rtr + "b"])
                        P.add("pool", lambda e: e.tensor_tensor(out=QT[hb][64:96, cg * 512:(cg + 1) * 512], in0=rt[64:96, 0, :], in1=rt[64:96, 1, :], op=ALU.add),
                              reads=[rtr + "a", rtr + "b"], writes=[qr + "r"])
                    pieces.append(p_a)
                    pieces.append(p_b)
                c0 = 0
                while c0 < NTOK:
                    cw = 256 if c0 == 0 else 512

                    def p_k(c0=c0, cw=cw):
                        bb = 6 + uim["i"] % 2
                        uim["i"] += 1
                        tl = [(c0 + q) // 128 for q in range(0, cw, 128)]

                        def mm(e):
                            ins = None
                            for k in range(2):
                                ins = e.matmul(banks[bb][0:64, 0:cw], lhsT=wkn[:, k, h, :], rhs=ckvnT[:, k, c0:c0 + cw], start=(k == 0), stop=(k == 1))
                            return ins
                        P.add("pe", mm, reads=["wkn"] + ["ckvnT_%d" % t for t in tl], writes=["bank%d" % bb])
                        P.add("dve", lambda e: e.tensor_copy(out=KT[hb][0:64, c0:c0 + cw], in_=banks[bb][0:64, 0:cw]),
                              reads=["bank%d" % bb], writes=["KT%d_n%d" % (hb, c0)])
                        if c0 == 0:
                            P.add("pool", lambda e: e.tensor_copy(out=KT[hb][64:96, :], in_=KRT[64:96, :]), reads=["KRT"], writes=["KT%d_r" % hb])
                    pieces.append(p_k)
                    c0 += cw
                for t8 in range(0, 18, 8):
                    def p_v(t8=t8):
                        tn = min(8, 18 - t8)
                        bb = 6 + uim["i"] % 2
                        uim["i"] += 1

                        def mm(e):
                            ins = None
                            for q in range(tn):
                                for k in range(2):
                                    ins = e.matmul(banks[bb][:, q * 64:(q + 1) * 64], lhsT=ckvnT[:, k, (t8 + q) * 128:(t8 + q + 1) * 128], rhs=wv[:, k, h, :],
                                                   start=(k == 0), stop=(k == 1))
                            return ins
                        P.add("pe", mm, reads=["wv"] + ["ckvnT_%d" % (t8 + q) for q in range(tn)], writes=["bank%d" % bb])
                        P.add("dve", lambda e: e.tensor_copy(
                            out=vaugm[hb][:, t8:t8 + tn, hb * 64:(hb + 1) * 64], in_=banks[bb][:, 0:tn * 64].rearrange("p (a b) -> p a b", a=tn)),
                            reads=["bank%d" % bb], writes=["vaugm%d" % hb])
                    pieces.append(p_v)
                return pieces

            munits = [(h, qg) for h in range(8) for qg in range(4)]

            def mla_qk(ui, kc):
                h, qg = munits[ui]
                hb = h % 2
                ptb = PT[ui % 2]
                ptn = "PT%d" % (ui % 2)
                bb = uim["i"] % 4
                uim["i"] += 1
                ktres = ["KT%d_r" % hb] + ["KT%d_n%d" % (hb, c) for c in (0, 256, 768, 1280, 1792)]
                P.add("pe", lambda e: e.matmul(
                    banks[bb][:, :], lhsT=KT[hb][0:96, kc * 128:(kc + 1) * 128], rhs=QT[hb][0:96, qg * 512:(qg + 1) * 512], start=True, stop=True),
                    reads=ktres + ["QT%d_%dn" % (hb, qg), "QT%d_%dr" % (hb, qg)], writes=["bank%d" % bb])
                P.add("act", lambda e: e.activation(out=ptb[:, kc, :], in_=banks[bb][:, :], func=AF.Exp, scale=SC),
                      reads=["bank%d" % bb], writes=["%s_%d" % (ptn, kc)])

            def mla_pv(ui, kc):
                h, qg = munits[ui]
                hb = h % 2
                ptb = PT[ui % 2]
                ptn = "PT%d" % (ui % 2)
                bO = 4 + ui % 2
                P.add("pe", lambda e: e.matmul(banks[bO][:, :], lhsT=vaugm[hb][:, kc, :], rhs=ptb[:, kc, :], start=(kc == 0), stop=(kc == 17)),
                      reads=["%s_%d" % (ptn, kc), "vaugm%d" % hb], writes=["bank%d" % bO])

            def mla_fin(ui):
                h, qg = munits[ui]
                hb = h % 2
                bO = 4 + ui % 2
                orow = slice(0, 64) if hb == 0 else slice(64, 128)
                srow = slice(64, 128) if hb == 0 else slice(0, 64)
                rec, recr = rec_rot.next()
                P.add("dve", lambda e: e.reciprocal(out=rec[orow, :], in_=banks[bO][srow, :]), reads=["bank%d" % bO], writes=[recr])
                P.add("dve", lambda e: e.tensor_tensor(out=oT[orow, 4 + h // 2, qg * 512:(qg + 1) * 512], in0=banks[bO][orow, :], in1=rec[orow, :], op=ALU.mult),
                      reads=["bank%d" % bO, recr], writes=["oT_%d_%d_%d" % (4 + h // 2, qg, hb)])

            for pc_ in mla_proj_pieces(0):
                pc_()
            NU = len(munits)
            pend_p = []
            stepc = 0
            for ui in range(NU + 1):
                if ui < NU and munits[ui][1] == 0:
                    for pc_ in pend_p:
                        pc_()
                    pend_p = []
                    if munits[ui][0] + 1 < 8:
                        pend_p = mla_proj_pieces(munits[ui][0] + 1)
                for kc in range(18):
                    if ui < NU:
                        mla_qk(ui, kc)
                    if ui >= 1:
                        mla_pv(ui - 1, kc)
                    stepc += 1
                    if pend_p and stepc % 4 == 2:
                        pend_p.pop(0)()
                if ui >= 1:
                    mla_fin(ui - 1)
            P.barrier()
            A.release(mk2)

            A.release(mkb)
            oT_keep = A.alloc([128, 8, SEQ], BF16)
            wgate = A.alloc([128, 8, 2, 8, 128], BF16)
            wo = A.alloc([128, 8, 2, 4, 128], BF16)
            wout = A.alloc([128, 8, D], BF16)
            C3 = A.alloc([128, D], F32)
            hTg = [A.alloc([128, 8, 512], BF16) for _ in range(2)]
            yT = [A.alloc([128, 8, 512], BF16) for _ in range(2)]
            x_rot = Rot(A, "mx", 3, [128, D], F32)
            xm_rot = Rot(A, "mxd", 3, [128, D], F32)
            mpend = [None]
            xn_rot = Rot(A, "mxn", 3, [128, D], BF16)
            yt_rot = Rot(A, "myt", 2, [128, D], F32)
            sg_rot = Rot(A, "msg", 4, [128, 512], F32)
            y12_rot = Rot(A, "my12", 4, [128, 512], F32)
            for dc in range(8):
                dma("pool", wgate[:, dc], wgate_d[dc], writes=["wgate%d" % dc], chan="ld_wm%d" % dc)
                dma("pool", wo[:, dc], wo_d[dc], writes=["wo%d" % dc], chan="ld_wn%d" % dc)
            dma("pool", wout, wout_d, writes=["wout0", "wout1"], chan="ld_wout")
            make_bc(C3, "C3", Cc[:, b, 1, :])
            ui = 0
            trb["banks"] = (4, 5, 6, 7)

            def merge_pn_steps(g):
                hg = hTg[g % 2]
                items = []
                for tt in range(4):
                    t = 2 + g * 4 + tt
                    items.append((x1s[t * 128:(t + 1) * 128, :], "x1s_%d" % t, ABc[:, b, 1, 0, :], ABc[:, b, 1, 1, :],
                                  (lambda k, tt=tt, hg=hg: hg[:, k, tt * 128:(tt + 1) * 128]), "hTg%d_%d" % (g % 2, tt)))
                return pn_steps(items, x_rot, xn_rot)
            for g in range(4):
                hg = hTg[g % 2]
                yg = yT[g % 2]
                if g == 0:
                    for st_ in merge_pn_steps(0):
                        st_()
                hres = ["hTg%d_%d" % (g % 2, tt) for tt in range(4)]
                nxt_steps = merge_pn_steps(g + 1) if g + 1 < 4 else []
                for dc in range(8):
                    bs = [0, 1, 2, 3]
                    ui += 1
                    if 1 <= dc <= 6 and nxt_steps:
                        nxt_steps[dc - 1]()
                    for ab in range(2):
                        def mmg(e, ab=ab, dc=dc, hg=hg):
                            ins = None
                            for k in range(8):
                                ins = e.matmul(banks[bs[ab]][:, :], lhsT=wgate[:, dc, ab, k, :], rhs=hg[:, k, :], start=(k == 0), stop=(k == 7))
                            return ins
                        P.add("pe", mmg, reads=hres + ["wgate%d" % dc], writes=["bank%d" % bs[ab]])
                    for ab in range(2):
                        def mmz(e, ab=ab, dc=dc, g=g):
                            ins = None
                            for k in range(4):
                                ins = e.matmul(banks[bs[2 + ab]][:, :], lhsT=wo[:, dc, ab, k, :], rhs=oT_keep[:, ab * 4 + k, g * 512:(g + 1) * 512],
                                               start=(k == 0), stop=(k == 3))
                            return ins
                        ores = ["oT_%d_%d_%d" % (c, g, e_) for c in range(ab * 4, ab * 4 + 4) for e_ in range(2)]
                        P.add("pe", mmz, reads=ores + ["wo%d" % dc], writes=["bank%d" % bs[2 + ab]])
                    y12 = []
                    for ab in range(2):
                        sg, sgr = sg_rot.next()
                        yy, yyr = y12_rot.next()
                        P.add("act", lambda e, sg=sg, bq=bs[ab], ab=ab, dc=dc: e.activation(out=sg, in_=banks[bq][:, :], func=AF.Sigmoid, bias=bgc[:, ab, dc:dc + 1]),
                              reads=["bank%d" % bs[ab], "bgc"], writes=[sgr])
                        P.add("dve", lambda e, sg=sg, yy=yy, bq=bs[2 + ab]: e.tensor_tensor(out=yy, in0=sg, in1=banks[bq][:, :], op=ALU.mult),
                              reads=[sgr, "bank%d" % bs[2 + ab]], writes=[yyr])
                        y12.append((yy, yyr))
                    P.add("pool", lambda e, y12=y12, yg=yg, dc=dc: e.tensor_tensor(out=yg[:, dc, :], in0=y12[0][0], in1=y12[1][0], op=ALU.add),
                          reads=[y12[0][1], y12[1][1]], writes=["yT%d_%d" % (g % 2, dc)])
                for tt in range(4):
                    t = 2 + g * 4 + tt
                    yb = (4, 5) if tt % 2 == 0 else (6, 7)
                    xt, xr = xm_rot.next()
                    dma("sp", xt, x1s[t * 128:(t + 1) * 128, :], reads=["x1s_%d" % t], writes=[xr], chan="ld_" + xr)
                    if mpend[0] is not None:
                        pd = mpend[0]
                        dma("pool", pd[0], pd[1], reads=[pd[2]], writes=[pd[3]], chan="st_" + pd[2])
                        mpend[0] = None
                    for hh in range(2):
                        def mm(e, hh=hh, yb=yb, tt=tt, yg=yg):
                            ins = None
                            for k in range(8):
                                ins = e.matmul(banks[yb[hh]][:, :], lhsT=yg[:, k, tt * 128:(tt + 1) * 128], rhs=wout[:, k, hh * 512:(hh + 1) * 512],
                                               start=(k == 0), stop=(k == 7))
                            return ins
                        P.add("pe", mm, reads=["yT%d_%d" % (g % 2, dc) for dc in range(8)] + ["wout%d" % hh], writes=["bank%d" % yb[hh]])
                    epilogue(yb, xt, xr, C3, "C3", yt_rot)
                    tl = g * 4 + tt
                    dma("pool", x2s[tl * 128:(tl + 1) * 128, :], xt, reads=[xr], writes=["x2s_%d" % tl], chan="st_" + xr)
            trb["banks"] = (2, 3)
            P.barrier()
            A.release(mkb)

            g_l = []
            for g in range(2):
                g_l.append((b, [(x2s[t * 128:(t + 1) * 128, :], out_d[b, t * 128:(t + 1) * 128, :], "x2s_%d" % t, "out_%d_%d" % (b, t), b)
                                for t in range(g * 8, g * 8 + 8)]))
            ffn_phase(1, g_l, 2, 2)

        P.emit()
        print("arena peak bytes:", A.peak, "ops:", {e: len(P.ops[e]) for e in ENGS})
    return nc


def _partner():
    p = np.zeros(32, np.int64)
    for d in range(32):
        ax, dd = d // 16, d % 16
        p[d] = ax * 16 + (dd + 8 if dd < 8 else dd - 8)
    return p


def _rope_table():
    half = 8
    freqs = (np.float32(10000.0) ** (-np.arange(half, dtype=np.float32) / np.float32(half))).astype(np.float32)
    t = np.arange(SEQ)
    rows = (t // 64).astype(np.float32)
    cols = (t % 64).astype(np.float32)
    ang_r = rows[:, None] * freqs
    ang_c = cols[:, None] * freqs
    tab = np.zeros((2, 32, NTOK), np.float32)
    tab[0, :, :CTX] = 1.0
    for d in range(32):
        ax, dd = d // 16, d % 16
        ang = (ang_r if ax == 0 else ang_c)[:, dd % 8].astype(np.float32)
        tab[0, d, CTX:] = np.cos(ang)
        tab[1, d, CTX:] = (-np.sin(ang)) if dd < 8 else np.sin(ang)
    return tab


def _na_bias(rpb):
    kc = np.arange(64)
    qc = np.arange(64)
    ws = np.clip(qc - 8, 0, 48)
    colvalid = (kc[:, None] >= ws[None, :]) & (kc[:, None] < ws[None, :] + 16)
    coloff = np.clip(kc[:, None] - qc[None, :], -15, 15) + 15
    negt = np.full((8, 64, 64), NEG, np.float32)

    def tile(dc, middle):
        out = np.full((8, 2, 64, 2, 64), NEG, np.float32)
        for kr in range(2):
            for qr in range(2):
                dr = 2 * dc + kr - qr
                ri = dr + 7
                if ri < 0 or ri > 14:
                    continue
                if middle and not (-4 <= dr <= 3):
                    continue
                vals = rpb[:, ri, :][:, coloff]
                out[:, kr, :, qr, :] = np.where(colvalid[None], vals, negt)
        return out.reshape(8, 128, 128)
    tiles = [tile(-3, False), tile(-2, False), tile(-1, True), tile(0, True), tile(1, True), tile(2, False), tile(3, False),
             tile(2, True), tile(1, True), tile(0, True), tile(-1, True), tile(-2, True)]
    return np.ascontiguousarray(np.stack(tiles, 1).transpose(0, 2, 1, 3))


def _prep_shared(inp):
    f = lambda a: np.ascontiguousarray(a, dtype=np.float32)
    sh = {}
    w_ada = inp["w_ada"][0]
    sh["w_ada_t"] = f(w_ada.reshape(8, 128, 18, 512).transpose(2, 1, 0, 3))
    sh["b_ada_cols"] = f(inp["b_ada"][0].reshape(72, 128).T)
    sh["norm_g_cols"] = f(inp["norm_g"][0].reshape(6, 8, 128).transpose(2, 0, 1))
    for n, p in (("f1", "ffn1"), ("f2", "ffn2")):
        a = np.stack([inp[p + "_w1"][0], inp[p + "_w3"][0]], 0)
        sh[n + "_w13"] = f(a.reshape(2, 8, 128, 22, 128).transpose(3, 2, 0, 1, 4))
        sh[n + "_w2"] = f(inp[p + "_w2"][0].reshape(11, 2, 128, D).transpose(0, 2, 1, 3))
    w_in = inp["w_in"][0]
    qa = w_in[:, 0:512].reshape(8, 128, 4, 128)
    ka = w_in[:, 512:1024].reshape(8, 128, 4, 128)
    sh["w_qk"] = f(np.stack([qa, ka], 0).transpose(3, 2, 0, 1, 4))
    sh["w_va"] = f(w_in[:, 1024:1536].reshape(8, 128, 4, 128).transpose(2, 1, 0, 3))
    sh["w_lora"] = f(w_in[:, 1536:2560].reshape(8, 128, 1024).transpose(1, 0, 2))
    part = _partner()
    kr = w_in[:, 2560:2592]
    wkr = np.zeros((128, 8, 2, 96), np.float32)
    wkr[:, :, 0, 64:96] = kr.reshape(8, 128, 32).transpose(1, 0, 2)
    wkr[:, :, 1, 64:96] = kr[:, part].reshape(8, 128, 32).transpose(1, 0, 2)
    sh["w_kr"] = wkr
    ga = w_in[:, 2592:3616]
    gb = w_in[:, 3616:4640]
    sh["w_gate"] = f(np.stack([ga, gb], 0).reshape(2, 8, 128, 8, 128).transpose(3, 2, 0, 1, 4))
    sh["b_gate_cols"] = f(inp["b_gate"][0].reshape(2, 8, 128).transpose(2, 0, 1))
    sh["g_q_cols"] = f(inp["g_q_lora"][0].reshape(6, 128).T)
    sh["g_kv_cols"] = f(inp["g_kv_lora"][0].reshape(2, 128).T)
    w_uq = inp["w_uq"][0]
    wuq = np.zeros((8, 128, 6, 2, 96), np.float32)
    for h in range(8):
        a_ = w_uq[:, h * 96:(h + 1) * 96]
        wuq[h, :, :, 0, :] = a_.reshape(6, 128, 96).transpose(1, 0, 2)
        r_ = w_uq[:, h * 96 + 64 + part]
        wuq[h, :, :, 1, 64:96] = r_.reshape(6, 128, 32).transpose(1, 0, 2)
    sh["w_uq_t"] = wuq
    w_ukv = inp["w_ukv"][0].reshape(2, 128, 8, 128)
    sh["w_kn"] = f(w_ukv[:, :, :, 0:64].transpose(1, 0, 2, 3))
    sh["w_v"] = f(w_ukv[:, :, :, 64:128].transpose(1, 0, 2, 3))
    sh["rope_tab"] = _rope_table()
    sh["na_bias"] = _na_bias(np.asarray(inp["rpb"][0], np.float32))
    wo = np.stack([inp["w_o_na"][0], inp["w_o_mla"][0]], 0)
    sh["w_o"] = f(wo.reshape(2, 4, 128, 8, 128).transpose(3, 2, 0, 1, 4))
    sh["w_out_t"] = f(inp["w_out"][0].reshape(8, 128, D).transpose(1, 0, 2))
    sh["ident"] = np.eye(128, dtype=np.float32)
    pp = np.zeros((32, 96), np.float32)
    for m in range(32):
        pp[part[m], 64 + m] = 1.0
    sh["permp"] = pp
    return sh


def _in_maps(inp):
    inp = {k: np.asarray(v) for k, v in inp.items()}
    sh = _prep_shared(inp)
    maps = []
    for c in range(NCORES):
        m = dict(sh)
        b0 = c * NB
        m["x"] = np.ascontiguousarray(inp["x"][b0:b0 + NB], dtype=np.float32)
        m["ctx"] = np.ascontiguousarray(inp["ctx"][b0:b0 + NB], dtype=np.float32)
        cs = np.stack([inp["c"][b0], inp["c"][b0 + 1], inp["c_ctx"]], 0)
        m["ccols"] = np.ascontiguousarray(cs.reshape(3, 8, 128).transpose(2, 1, 0), dtype=np.float32)
        maps.append(m)
    return maps


_NC_CACHE = {}


def kernel(**inputs):
    if "nc" not in _NC_CACHE:
        _NC_CACHE["nc"] = build_program()
    nc = _NC_CACHE["nc"]
    maps = _in_maps(inputs)
    res = run_bass_kernel_spmd(nc, maps, core_ids=list(range(NCORES)))
    out = np.concatenate([np.asarray(r["out"]) for r in res.results], axis=0)
    return out.astype(np.float32)
```

```python
import contextlib
import numpy as np
import concourse.bass as bass
import concourse.mybir as mybir
from concourse.bass_utils import run_bass_kernel_spmd

F32 = mybir.dt.float32
BF16 = mybir.dt.bfloat16
AF = mybir.ActivationFunctionType
ALU = mybir.AluOpType

ENGS = ("pe", "act", "dve", "pool", "sp")
NCORES = 8
NB = 2
D = 1024
FF = 2816
SEQ = 2048
CTX = 256
NTOK = SEQ + CTX
EPS = 1e-6
NEG = -30000.0
import os
LORA3 = os.environ.get("K_LORA3", "1") == "1"
ADABG = os.environ.get("K_ADABG", "0") == "1"


class Op:
    __slots__ = ("eng", "fn", "deps", "idx", "needed", "sigval", "chan")

    def __init__(self, eng, fn, chan=None):
        self.eng = eng
        self.fn = fn
        self.deps = {}
        self.needed = False
        self.sigval = 0
        self.chan = chan
        self.idx = -1


class Prog:
    def __init__(self, nc):
        self.nc = nc
        self.ops = {e: [] for e in ENGS}
        self.res_w = {}
        self.res_r = {}
        self.pending = {e: {} for e in ENGS}
        self.chan_last = {}
        self.chans = []

    @staticmethod
    def _key(op):
        return op.chan if op.chan is not None else op.eng

    def _adddep(self, op, d, kind):
        if d is op:
            return
        if d.chan is None and d.eng == op.eng and op.chan is None:
            if op.eng == "pe" or (kind == "R" and op.eng != "pool"):
                return
        k = self._key(d)
        cur = op.deps.get(k)
        if cur is None or cur.idx < d.idx:
            op.deps[k] = d

    def add(self, eng, fn, reads=(), writes=(), chan=None):
        op = Op(eng, fn, chan)
        lst = self.ops[eng]
        if chan is not None:
            if chan not in self.chan_last:
                self.chans.append(chan)
            prev = self.chan_last.get(chan)
            op.idx = (prev.idx + 1) if prev is not None else 1
            self.chan_last[chan] = op
        else:
            op.idx = len(lst)
        for r in reads:
            w = self.res_w.get(r)
            if w is not None:
                self._adddep(op, w, "W")
        for w_ in writes:
            lw = self.res_w.get(w_)
            if lw is not None:
                self._adddep(op, lw, "W")
            rd = self.res_r.get(w_)
            if rd:
                for d in rd.values():
                    self._adddep(op, d, "R")
        for k, d in self.pending[eng].items():
            if d is not op:
                cur = op.deps.get(k)
                if cur is None or cur.idx < d.idx:
                    op.deps[k] = d
        self.pending[eng] = {}
        for d in op.deps.values():
            d.needed = True
        for r in reads:
            self.res_r.setdefault(r, {})[self._key(op)] = op
        for w_ in writes:
            self.res_w[w_] = op
            self.res_r[w_] = {}
        lst.append(op)
        return op

    def barrier(self):
        last = {}
        for e in ENGS:
            for op in reversed(self.ops[e]):
                if op.chan is None:
                    last[e] = op
                    break
        for c, op in self.chan_last.items():
            last[c] = op
        for e in ENGS:
            for k, d in last.items():
                if k == e and e != "pool":
                    continue
                cur = self.pending[e].get(k)
                if cur is None or cur.idx < d.idx:
                    self.pending[e][k] = d

    def emit(self, final_waits_eng="sp"):
        nc = self.nc
        with contextlib.ExitStack() as st:
            sems = {}
            for e in ENGS:
                sems[e] = st.enter_context(nc.semaphore("s_" + e))
            for c in self.chans:
                sems[c] = st.enter_context(nc.semaphore("c_" + str(c)))
            for e in ENGS:
                cnt = 0
                for op in self.ops[e]:
                    if op.chan is None and op.needed:
                        cnt += 1
                        op.sigval = cnt
                    elif op.chan is not None:
                        op.sigval = 16 * op.idx
            print("semaphores used:", len(sems))
            block = st.enter_context(nc.Block())
            handles = {"pe": "tensor", "act": "scalar", "dve": "vector", "pool": "gpsimd", "sp": "sync"}

            def make(e):
                def body(eng):
                    waited = {}
                    for op in self.ops[e]:
                        for k, d in op.deps.items():
                            v = d.sigval
                            assert v > 0, (e, k, d.eng, d.chan, d.idx)
                            if waited.get(k, 0) >= v:
                                continue
                            waited[k] = v
                            eng.wait_ge(sems[k], v)
                        ins = op.fn(eng)
                        if op.chan is not None:
                            ins.then_inc(sems[op.chan], 16)
                        elif op.needed:
                            ins.then_inc(sems[e], 1)
                    if e == final_waits_eng:
                        for c, op in self.chan_last.items():
                            v = op.sigval
                            if waited.get(c, 0) >= v:
                                continue
                            eng.wait_ge(sems[c], v)
                return body

            for e in ENGS:
                if not self.ops[e] and e != final_waits_eng:
                    continue
                getattr(block, handles[e])(make(e))


class Arena:
    def __init__(self, ar, nbytes):
        self.ar = ar
        self.nbytes = nbytes
        self.off = 0
        self.peak = 0
        self.top = nbytes

    def alloc(self, shape, dtype):
        esz = 4 if dtype == F32 else 2
        n = 1
        for s in shape[1:]:
            n *= s
        size = (n * esz + 63) // 64 * 64
        assert self.off + size <= min(self.nbytes, self.top if self.off < self.top else self.nbytes), ("SBUF arena overflow", self.off, size, self.nbytes, self.top)
        a = self.ar[:, self.off // 4:(self.off + size) // 4]
        if dtype != F32:
            a = a.bitcast(dtype)
        a = a[:, 0:n]
        if len(shape) == 3:
            a = a.rearrange("p (a b) -> p a b", a=shape[1])
        elif len(shape) == 4:
            a = a.rearrange("p (a b c) -> p a b c", a=shape[1], b=shape[2])
        elif len(shape) == 5:
            a = a.rearrange("p (a b c d) -> p a b c d", a=shape[1], b=shape[2], c=shape[3])
        self.off += size
        self.peak = max(self.peak, self.off)
        return a

    def alloc_top(self, shape, dtype):
        esz = 4 if dtype == F32 else 2
        n = 1
        for s in shape[1:]:
            n *= s
        size = (n * esz + 63) // 64 * 64
        save = self.off
        self.top -= size
        assert self.top >= self.off, ("SBUF arena overflow (top)", self.off, self.top)
        self.off = self.top
        lim = self.nbytes
        self.nbytes = self.top + size
        a = self.alloc(shape, dtype)
        self.nbytes = lim
        self.off = save
        return a

    def free_top(self):
        self.top = self.nbytes

    def mark(self):
        return self.off

    def release(self, m):
        self.off = m


class Rot:
    def __init__(self, arena, name, n, shape, dtype):
        self.bufs = [arena.alloc(shape, dtype) for _ in range(n)]
        self.name = name
        self.i = 0

    def next(self):
        j = self.i % len(self.bufs)
        self.i += 1
        return self.bufs[j], "%s%d" % (self.name, j)


def build_program(dbg=None):
    nc = bass.Bass("TRN2", target_bir_lowering=False)
    di = lambda n, s: nc.dram_tensor(n, list(s), F32, kind="ExternalInput").ap()
    x_d = di("x", (NB, SEQ, D))
    ctx_d = di("ctx", (NB, CTX, D))
    ccols_d = di("ccols", (128, 8, 3))
    wada_d = di("w_ada_t", (18, 128, 8, 512))
    bada_d = di("b_ada_cols", (128, 72))
    ng_d = di("norm_g_cols", (128, 6, 8))
    f_w13_d = [di("f1_w13", (22, 128, 2, 8, 128)), di("f2_w13", (22, 128, 2, 8, 128))]
    f_w2_d = [di("f1_w2", (11, 128, 2, D)), di("f2_w2", (11, 128, 2, D))]
    wqk_d = di("w_qk", (4, 128, 2, 8, 128))
    wva_d = di("w_va", (4, 128, 8, 128))
    wlora_d = di("w_lora", (128, 8, 1024))
    wkr_d = di("w_kr", (128, 8, 2, 96))
    wgate_d = di("w_gate", (8, 128, 2, 8, 128))
    bgate_d = di("b_gate_cols", (128, 2, 8))
    gq_d = di("g_q_cols", (128, 6))
    gkv_d = di("g_kv_cols", (128, 2))
    wuq_d = di("w_uq_t", (8, 128, 6, 2, 96))
    wkn_d = di("w_kn", (128, 2, 8, 64))
    wv_d = di("w_v", (128, 2, 8, 64))
    rope_d = di("rope_tab", (2, 32, NTOK))
    nab_d = di("na_bias", (8, 128, 12, 128))
    wo_d = di("w_o", (8, 128, 2, 4, 128))
    wout_d = di("w_out_t", (128, 8, D))
    ident_d = di("ident", (128, 128))
    permp_d = di("permp", (32, 96))
    out_d = nc.dram_tensor("out", [NB, SEQ, D], F32, kind="ExternalOutput").ap()
    x1s = nc.dram_tensor("x1s", [NTOK, D], F32, kind="Internal").ap()
    x2s = nc.dram_tensor("x2s", [SEQ, D], F32, kind="Internal").ap()
    dbg_out = {}
    if dbg:
        for n, s in dbg.items():
            dbg_out[n] = nc.dram_tensor("dbg_" + n, list(s), F32, kind="ExternalOutput").ap()

    P = Prog(nc)
    ARENA_BYTES = 200 * 1024
    with contextlib.ExitStack() as st:
        ar_t = st.enter_context(nc.sbuf_tensor("arena", [128, ARENA_BYTES // 4], F32))
        A = Arena(ar_t[:, :], ARENA_BYTES)
        banks = [st.enter_context(nc.psum_tensor("bank%d" % i, [128, 512], F32)) for i in range(8)]
        bankb = [b[:, :].bitcast(BF16) for b in banks]

        ident = A.alloc([128, 128], BF16)
        identf = A.alloc([128, 128], F32)
        onesf = A.alloc([128, 128], F32)
        mT = A.alloc([128, 72, 3], F32)
        ngc = A.alloc([128, 6, 8], F32)
        badac = A.alloc([128, 72], F32)
        ABc = A.alloc([128, 3, 3, 2, 8], F32)
        Cc = A.alloc([128, 3, 3, 8], F32)
        bgc = A.alloc([128, 2, 8], F32)
        gqc = A.alloc([128, 6], F32)
        gkvc = A.alloc([128, 2], F32)
        ss_rot = Rot(A, "ss", 6, [128, 4], F32)
        rs_rot = Rot(A, "rs", 6, [128, 2], F32)
        junk = A.alloc([128, 1024], BF16)

        dq = {"n": 0}

        def dma(eng, out, in_, reads=(), writes=(), chan=None):
            if chan is None:
                chan = "d%d" % dq["n"]
                dq["n"] += 1
            return P.add(eng, lambda e: e.dma_start(out=out, in_=in_), reads=reads, writes=writes, chan=chan)

        dma("sp", identf, ident_d, writes=["identf"], chan="g0")
        dma("pool", ident, ident_d, writes=["ident"], chan="g1")
        dma("sp", ngc, ng_d, writes=["ngc"], chan="g2")
        dma("sp", badac, bada_d, writes=["badac"], chan="g3")
        dma("sp", bgc, bgate_d, writes=["bgc"], chan="g4")
        dma("sp", gqc, gq_d, writes=["gqc"], chan="g5")
        dma("sp", gkvc, gkv_d, writes=["gkvc"], chan="g6")
        P.add("dve", lambda e: e.memset(onesf, 1.0), writes=["onesf"])
        neghalf = A.alloc([128, 2], F32)
        P.add("dve", lambda e: e.memset(neghalf, -0.5), writes=["neghalf"])

        cc = A.alloc([128, 8, 3], F32)
        scT = A.alloc([128, 8, 3], BF16)
        dma("sp", cc, ccols_d, writes=["cc"], chan="g7")
        P.add("act", lambda e: e.activation(out=scT, in_=cc, func=AF.Silu), reads=["cc"], writes=["scT"])

        scTf = A.alloc([128, 8, 3], F32)
        P.add("act", lambda e: e.activation(out=scTf, in_=cc, func=AF.Silu), reads=["cc"], writes=["scTf"])

        def ada_block(blk, wb, wres, bank, f32path=False, wbf=None):
            if f32path:
                dma("sp", wbf, wada_d[blk], writes=[wres + "f"], chan="ld_" + wres + "f")
                P.add("act", lambda e: e.activation(out=wb, in_=wbf, func=AF.Copy), reads=[wres + "f"], writes=[wres])
            else:
                dma("pool", wb, wada_d[blk], writes=[wres], chan="ld_" + wres)
            rhs_t = scT

            def mm(e):
                ins = None
                for f in range(4):
                    for k in range(8):
                        ins = e.matmul(banks[bank][:, f * 3:f * 3 + 3], lhsT=wb[:, k, f * 128:(f + 1) * 128],
                                       rhs=rhs_t[:, k, :], start=(k == 0), stop=(k == 7))
                return ins
            P.add("pe", mm, reads=[wres, "scT", "scTf"], writes=["bank%d" % bank])
            psv = banks[bank][:, 0:12].rearrange("p (a b) -> p a b", a=4)
            for s3 in range(3):
                P.add("dve", lambda e, s3=s3: e.tensor_tensor(out=mT[:, blk * 4:(blk + 1) * 4, s3], in0=psv[:, :, s3],
                                                              in1=badac[:, blk * 4:(blk + 1) * 4], op=ALU.add),
                      reads=["bank%d" % bank, "badac"], writes=["mT_%d" % blk])

        NORMS = ((0, 0, 1), (2, 3, 4), (4, 6, 7))
        CS = ((1, 2, 0.5), (3, 5, 1.0), (5, 8, 0.5))

        def derive(norms, cs):
            for s in range(3):
                for n_ in norms:
                    gi, sh, sc_ = NORMS[n_]
                    P.add("dve", lambda e, s=s, n_=n_, gi=gi, sc_=sc_: e.scalar_tensor_tensor(
                        out=ABc[:, s, n_, 0, :], in0=mT[:, sc_ * 8:(sc_ + 1) * 8, s], scalar=1.0, in1=ngc[:, gi, :],
                        op0=ALU.add, op1=ALU.mult), reads=["mT_%d" % (2 * sc_), "mT_%d" % (2 * sc_ + 1), "ngc"], writes=["ABc"])
                    P.add("dve", lambda e, s=s, n_=n_, sh=sh: e.tensor_copy(
                        out=ABc[:, s, n_, 1, :], in_=mT[:, sh * 8:(sh + 1) * 8, s]),
                        reads=["mT_%d" % (2 * sh), "mT_%d" % (2 * sh + 1)], writes=["ABc"])
                for w_ in cs:
                    gi, gt, wt = CS[w_]
                    P.add("dve", lambda e, s=s, w_=w_, gi=gi, gt=gt, wt=wt: e.scalar_tensor_tensor(
                        out=Cc[:, s, w_, :], in0=mT[:, gt * 8:(gt + 1) * 8, s], scalar=wt, in1=ngc[:, gi, :],
                        op0=ALU.mult, op1=ALU.mult), reads=["mT_%d" % (2 * gt), "mT_%d" % (2 * gt + 1), "ngc"], writes=["Cc"])

        mk = A.mark()
        wada = [A.alloc([128, 8, 512], BF16) for _ in range(2)]
        wadaf = [A.alloc([128, 8, 512], F32) for _ in range(2)]
        wadac = [A.alloc([128, 8, 512], BF16) for _ in range(2)]
        for blk in range(10 if ADABG else 18):
            if blk % 2 == 0:
                ada_block(blk, wada[(blk // 2) % 2], "wada%d" % ((blk // 2) % 2), blk % 4)
            else:
                ada_block(blk, wadac[(blk // 2) % 2], "wadac%d" % ((blk // 2) % 2), blk % 4, f32path=True, wbf=wadaf[(blk // 2) % 2])
        if ADABG:
            derive([0, 1], [0])
        else:
            derive([0, 1, 2], [0, 1, 2])
        P.barrier()
        A.release(mk)

        bcl = A.alloc([128, 8, 128], F32)

        def make_bc(dst, dst_res, col):
            for k in range(8):
                P.add("dve", lambda e, k=k: e.tensor_scalar(out=bcl[:, k, :], in0=onesf, scalar1=col[:, k:k + 1],
                                                            scalar2=None, op0=ALU.mult),
                      reads=["onesf", "Cc"], writes=["bcl%d" % k])
            for h in range(2):
                def mm(e, h=h):
                    ins = None
                    for kk in range(4):
                        k = h * 4 + kk
                        ins = e.matmul(banks[6 + h][:, kk * 128:(kk + 1) * 128], lhsT=bcl[:, k, :], rhs=identf,
                                       start=True, stop=True)
                    return ins
                P.add("pe", mm, reads=["bcl%d" % (h * 4 + kk) for kk in range(4)] + ["identf"], writes=["bank%d" % (6 + h)])
                P.add("act", lambda e, h=h: e.activation(out=dst[:, h * 512:(h + 1) * 512], in_=banks[6 + h][:, :], func=AF.Copy),
                      reads=["bank%d" % (6 + h)], writes=[dst_res])

        trb = {"i": 0, "banks": (2, 3)}

        def pn_a(xt, xt_res):
            ss, ssr = ss_rot.next()
            rs, rsr = rs_rot.next()
            P.add("act", lambda e: e.activation(out=junk, in_=xt, func=AF.Square, accum_out=ss[:, 0:1]),
                  reads=[xt_res], writes=[ssr, "junk"])
            P.add("dve", lambda e: e.tensor_scalar(out=rs[:, 0:1], in0=ss[:, 0:1], scalar1=1.0 / D, scalar2=EPS,
                                                   op0=ALU.mult, op1=ALU.add), reads=[ssr], writes=[rsr])
            P.add("pool", lambda e: e.tensor_tensor(out=rs[:, 1:2], in0=rs[:, 0:1], in1=neghalf[:, 0:1], op=ALU.pow),
                  reads=[rsr, "neghalf"], writes=[rsr + "p"])
            return (xt, xt_res, rs, rsr)

        def pn_c(st_, xn_rot):
            xt, xt_res, rs, rsr = st_
            xn, xnr = xn_rot.next()
            P.add("act", lambda e: e.activation(out=xn, in_=xt, func=AF.Copy, scale=rs[:, 1:2]),
                  reads=[xt_res, rsr + "p"], writes=[xnr])
            return (xn, xnr)

        def pn_t(st2, Acol, Bcol, dst_fn, dst_res):
            xn, xnr = st2
            bi = trb["banks"][trb["i"] % len(trb["banks"])]
            trb["i"] += 1
            pb = bankb[bi]

            def tr(e):
                ins = None
                for k in range(8):
                    ins = e.transpose(out=pb[:, k * 128:(k + 1) * 128], in_=xn[:, k * 128:(k + 1) * 128], identity=ident)
                return ins
            P.add("pe", tr, reads=[xnr, "ident"], writes=["bank%d" % bi])
            for k in range(8):
                P.add("dve", lambda e, k=k: e.tensor_scalar(out=dst_fn(k), in0=pb[:, k * 128:(k + 1) * 128],
                                                            scalar1=Acol[:, k:k + 1], scalar2=Bcol[:, k:k + 1],
                                                            op0=ALU.mult, op1=ALU.add),
                      reads=["bank%d" % bi, "ABc"], writes=[dst_res])

        def pn_steps(items, xrot, xn_rot):
            state = {}
            steps = []

            state2 = {}

            def mk(j):
                def step():
                    if j < len(items):
                        src, sres, Ac, Bc, dfn, dres = items[j]
                        xt, xr = xrot.next()
                        dma("sp", xt, src, reads=[sres], writes=[xr], chan="ld_" + xr)
                        state[j] = pn_a(xt, xr)
                    if 1 <= j <= len(items):
                        state2[j - 1] = pn_c(state.pop(j - 1), xn_rot)
                    if 2 <= j:
                        src, sres, Ac, Bc, dfn, dres = items[j - 2]
                        pn_t(state2.pop(j - 2), Ac, Bc, dfn, dres)
                return step
            for j in range(len(items) + 2):
                steps.append(mk(j))
            return steps

        def epilogue(y_banks, xt, xt_res, Cbc, Cres, ytmp_rot, nfeat=D):
            ss, ssr = ss_rot.next()
            rs, rsr = rs_rot.next()
            yt, ytr = ytmp_rot.next()
            for h in range(2):
                P.add("act", lambda e, h=h: e.activation(out=junk[:, h * 512:(h + 1) * 512], in_=banks[y_banks[h]][:, :],
                                                         func=AF.Square, accum_out=ss[:, h:h + 1]),
                      reads=["bank%d" % y_banks[h]], writes=[ssr + "_%d" % h, "junk"])
            P.add("dve", lambda e: e.tensor_scalar(out=rs[:, 0:1], in0=ss[:, 0:1], scalar1=ss[:, 1:2], scalar2=1.0 / nfeat,
                                                   op0=ALU.add, op1=ALU.mult), reads=[ssr + "_0", ssr + "_1"], writes=[rsr])
            P.add("dve", lambda e: e.tensor_scalar(out=rs[:, 0:1], in0=rs[:, 0:1], scalar1=EPS, scalar2=None,
                                                   op0=ALU.add), reads=[rsr], writes=[rsr])
            P.add("pool", lambda e: e.tensor_tensor(out=rs[:, 1:2], in0=rs[:, 0:1], in1=neghalf[:, 0:1], op=ALU.pow),
                  reads=[rsr, "neghalf"], writes=[rsr + "p"])
            for h in range(2):
                P.add("dve", lambda e, h=h: e.scalar_tensor_tensor(
                    out=yt[:, h * 512:(h + 1) * 512], in0=banks[y_banks[h]][:, :], scalar=rs[:, 1:2],
                    in1=Cbc[:, h * 512:(h + 1) * 512], op0=ALU.mult, op1=ALU.mult),
                    reads=["bank%d" % y_banks[h], rsr + "p", Cres], writes=[ytr + "_%d" % h])
            P.add("pool", lambda e: e.tensor_tensor(out=xt, in0=xt, in1=yt, op=ALU.add),
                  reads=[xt_res, ytr + "_0", ytr + "_1"], writes=[xt_res])

        def ffn_phase(fi, groups, normidx, cidx):
            mk = A.mark()
            w2 = A.alloc([128, 22, D], BF16)
            NW = 4
            w13 = [A.alloc([128, 2, 8, 128], BF16) for _ in range(NW)]
            TMAX = max(len(g[1]) for g in groups) * 128
            aT = A.alloc([128, 22, TMAX], BF16)
            hTs = [A.alloc([128, 8, TMAX], BF16) for _ in range(2)]
            x_rot = Rot(A, "fx", 3, [128, D], F32)
            xd_rot = Rot(A, "fxd", 3, [128, D], F32)
            xn_rot = Rot(A, "fxn", 3, [128, D], BF16)
            yt_rot = Rot(A, "fyt", 2, [128, D], F32)
            sl_rot = Rot(A, "fsl", 2, [128, 512], F32)
            streams = sorted(set(tl[4] for g in groups for tl in g[1]))
            Cbc = {}
            for s in streams:
                Cbc[s] = A.alloc([128, D], F32)
                make_bc(Cbc[s], "Cbc%d" % s, Cc[:, s, cidx, :])
            ub = {"i": 0}
            wst = {"issued": 0, "w2": 0}
            NCH = 22 * len(groups)

            def w_prefetch(upto):
                while wst["issued"] < min(upto, NCH):
                    gc = wst["issued"]
                    wr = "w13_%d" % (gc % NW)
                    dma("pool", w13[gc % NW], f_w13_d[fi][gc % 22], writes=[wr], chan="ld_" + wr)
                    wst["issued"] += 1
                    if gc >= 3 and gc % 2 == 1 and wst["w2"] < 11:
                        blk = wst["w2"]
                        dma("pool", w2[:, blk * 2:blk * 2 + 2, :], f_w2_d[fi][blk], writes=["w2_%d" % blk], chan="w2ld")
                        wst["w2"] += 1
            w_prefetch(NW)

            def group_pn_steps(gi):
                _, tiles = groups[gi]
                hT = hTs[gi % 2]
                items = []
                for tt, (src, dst, sres, dres, s) in enumerate(tiles):
                    items.append((src, sres, ABc[:, s, normidx, 0, :], ABc[:, s, normidx, 1, :],
                                  (lambda k, tt=tt, hT=hT: hT[:, k, tt * 128:(tt + 1) * 128]), "hT%d_%d" % (gi % 2, tt)))
                return pn_steps(items, x_rot, xn_rot)

            for st_ in group_pn_steps(0):
                st_()
            for gi, (_, tiles) in enumerate(groups):
                hT = hTs[gi % 2]
                hp = gi % 2
                nt = len(tiles)
                T = nt * 128
                ncg = (T + 511) // 512
                CW = T // ncg
                assert CW % 128 == 0 and CW * ncg == T
                for c in range(22):
                    gc = gi * 22 + c
                    w_prefetch(gc + NW)
                    wb = w13[gc % NW]
                    wr = "w13_%d" % (gc % NW)
                    j = 0
                    if True:
                        for cg in range(ncg):
                            b1 = ub["i"] % 2
                            b3 = 2 + ub["i"] % 2
                            ub["i"] += 1
                            hres = ["hT%d_%d" % (hp, t) for t in range(cg * CW // 128, (cg + 1) * CW // 128)]

                            def mm(e, wb=wb, j=j, cg=cg, b1=b1, b3=b3, CW=CW, hT=hT):
                                ins = None
                                for wi, bb in ((0, b1), (1, b3)):
                                    for k in range(8):
                                        ins = e.matmul(banks[bb][:, 0:CW], lhsT=wb[:, wi, k, :],
                                                       rhs=hT[:, k, cg * CW:(cg + 1) * CW], start=(k == 0), stop=(k == 7))
                                return ins
                            P.add("pe", mm, reads=[wr] + hres, writes=["bank%d" % b1, "bank%d" % b3])
                            sl, slr = sl_rot.next()
                            P.add("act", lambda e, sl=sl, b1=b1, CW=CW: e.activation(out=sl[:, 0:CW], in_=banks[b1][:, 0:CW], func=AF.Silu),
                                  reads=["bank%d" % b1], writes=[slr])
                            P.add("dve", lambda e, sl=sl, b3=b3, c=c, cg=cg, CW=CW: e.tensor_tensor(
                                out=aT[:, c, cg * CW:(cg + 1) * CW], in0=sl[:, 0:CW], in1=banks[b3][:, 0:CW], op=ALU.mult),
                                reads=[slr, "bank%d" % b3], writes=["aT_%d_%d" % (c, cg)])
                nxt = group_pn_steps(gi + 1) if gi + 1 < len(groups) else []
                per = (len(nxt) + nt - 1) // nt if nxt else 0
                ni = 0
                pend = None
                for tt, (src, dst, sres, dres, s) in enumerate(tiles):
                    yb = (4, 5) if tt % 2 == 0 else (6, 7)
                    cg = tt * 128 // CW
                    xt, xr = xd_rot.next()
                    dma("sp", xt, src, reads=[sres], writes=[xr], chan="ld_" + xr)
                    if pend is not None:
                        dma("pool", pend[0], pend[1], reads=[pend[2]], writes=[pend[3]], chan="st_" + pend[2])
                        pend = None
                    for h in range(2):
                        def mm(e, tt=tt, h=h, yb=yb):
                            ins = None
                            for c in range(22):
                                ins = e.matmul(banks[yb[h]][:, :], lhsT=aT[:, c, tt * 128:(tt + 1) * 128],
                                               rhs=w2[:, c, h * 512:(h + 1) * 512], start=(c == 0), stop=(c == 21))
                            return ins
                        P.add("pe", mm, reads=["aT_%d_%d" % (c, cg) for c in range(22)] + ["w2_%d" % b for b in range(11)],
                              writes=["bank%d" % yb[h]])
                    for _ in range(per):
                        if ni < len(nxt):
                            nxt[ni]()
                            ni += 1
                    epilogue(yb, xt, xr, Cbc[s], "Cbc%d" % s, yt_rot)
                    dma("pool", dst, xt, reads=[xr], writes=[dres], chan="st_" + xr)
                while ni < len(nxt):
                    nxt[ni]()
                    ni += 1
            P.barrier()
            A.release(mk)

        for b in range(NB):
            alltiles = [(ctx_d[b, t * 128:(t + 1) * 128, :], x1s[t * 128:(t + 1) * 128, :], "xin", "x1s_%d" % t, 2) for t in range(2)]
            alltiles += [(x_d[b, t * 128:(t + 1) * 128, :], x1s[256 + t * 128:256 + (t + 1) * 128, :], "xin", "x1s_%d" % (2 + t), b)
                         for t in range(16)]
            ffn_phase(0, [(None, alltiles[g * 6:(g + 1) * 6]) for g in range(3)], 0, 0)

            mkb = A.mark()
            oT = A.alloc([128, 8, SEQ], BF16)
            hT2 = A.alloc_top([128, 8, NTOK], BF16)
            mk2 = A.mark()
            p2_items = []
            for t in range(18):
                s = 2 if t < 2 else b
                p2_items.append((x1s[t * 128:(t + 1) * 128, :], "x1s_%d" % t, ABc[:, s, 1, 0, :], ABc[:, s, 1, 1, :],
                                 (lambda k, t=t: hT2[:, k, t * 128:(t + 1) * 128]), "hT2_%d" % t))

            wqk = [A.alloc([128, 2, 8, 128], BF16) for _ in range(2)]
            wva = [A.alloc([128, 8, 128], BF16) for _ in range(2)]
            qaT = [A.alloc([128, SEQ], BF16) for _ in range(2)]
            kaT = [A.alloc([128, NTOK], BF16) for _ in range(2)]
            vaug = [A.alloc([128, 18, 2, 128], BF16) for _ in range(2)]
            nab = [A.alloc([128, 12, 128], F32) for _ in range(2)]
            rec_rot = Rot(A, "nrec", 2, [128, 512], F32)
            px_rot = Rot(A, "px", 4, [128, D], F32)
            pxn_rot = Rot(A, "pxn", 3, [128, D], BF16)
            for i2 in range(2):
                P.add("pool", lambda e, i2=i2: e.memset(vaug[i2], 1.0), writes=["vaug%d_%d" % (i2, t) for t in range(18)])
            ubn = {"i": 0}

            def na_proj_pieces(pr):
                pb_ = pr % 2
                pieces = []

                def p_load():
                    dma("pool", wqk[pb_], wqk_d[pr], writes=["wqk%d" % pb_], chan="ld_wqk%d" % pb_)
                    dma("pool", wva[pb_], wva_d[pr], writes=["wva%d" % pb_], chan="ld_wva%d" % pb_)
                pieces.append(p_load)
                for which, dstT, ncols, t0 in ((0, qaT[pb_], SEQ, 2), (1, kaT[pb_], NTOK, 0)):
                    c0 = 0
                    while c0 < ncols:
                        cw = 256 if (which == 1 and c0 == 0) else 512

                        def p_qk(which=which, dstT=dstT, c0=c0, cw=cw, t0=t0):
                            bb = 6 + ubn["i"] % 2
                            ubn["i"] += 1
                            tl = [t0 + (c0 + q) // 128 for q in range(0, cw, 128)]
                            src0 = c0 + (256 if which == 0 else 0)

                            def mm(e):
                                ins = None
                                for k in range(8):
                                    ins = e.matmul(banks[bb][:, 0:cw], lhsT=wqk[pb_][:, which, k, :], rhs=hT2[:, k, src0:src0 + cw],
                                                   start=(k == 0), stop=(k == 7))
                                return ins
                            P.add("pe", mm, reads=["wqk%d" % pb_] + ["hT2_%d" % t for t in tl], writes=["bank%d" % bb])
                            rname = ("qaT%d_%d" if which == 0 else "kaT%d_%d") % (pb_, c0 // 512 if which == 0 else (0 if c0 == 0 else 1 + (c0 - 256) // 512))
                            P.add("dve", lambda e: e.tensor_copy(out=dstT[:, c0:c0 + cw], in_=banks[bb][:, 0:cw]),
                                  reads=["bank%d" % bb], writes=[rname])
                        pieces.append(p_qk)
                        c0 += cw
                for t4 in range(0, 18, 4):
                    def p_v(t4=t4):
                        tn = min(4, 18 - t4)
                        bb = 6 + ubn["i"] % 2
                        ubn["i"] += 1

                        def mm(e):
                            ins = None
                            for q in range(tn):
                                for k in range(8):
                                    ins = e.matmul(banks[bb][:, q * 128:(q + 1) * 128], lhsT=hT2[:, k, (t4 + q) * 128:(t4 + q + 1) * 128],
                                                   rhs=wva[pb_][:, k, :], start=(k == 0), stop=(k == 7))
                            return ins
                        P.add("pe", mm, reads=["wva%d" % pb_] + ["hT2_%d" % (t4 + q) for q in range(tn)], writes=["bank%d" % bb])
                        for e_ in range(2):
                            P.add("act", lambda e, e_=e_: e.activation(
                                out=vaug[pb_][:, t4:t4 + tn, e_, e_ * 64:(e_ + 1) * 64],
                                in_=banks[bb][:, 0:tn * 128].rearrange("p (a b) -> p a b", a=tn)[:, :, e_ * 64:(e_ + 1) * 64], func=AF.Copy),
                                reads=["bank%d" % bb], writes=["vaug%d_%d" % (pb_, t4 + q) for q in range(tn)])
                    pieces.append(p_v)
                return pieces

            def na_proj(pr):
                for pc_ in na_proj_pieces(pr):
                    pc_()

            nab8 = [A.alloc([128, 12, 128], BF16) for _ in range(2)]

            def na_bias_load(h):
                nbr = "nab%d" % (h % 2)
                dma("sp", nab[h % 2], nab_d[h], writes=[nbr], chan="ld_" + nbr)
                P.add("dve", lambda e: e.tensor_scalar(out=nab8[h % 2], in0=nab[h % 2], scalar1=8.0, scalar2=None, op0=ALU.mult),
                      reads=[nbr], writes=["nab8_%d" % (h % 2)])

            def js_of(i):
                if i <= 1:
                    return list(range(0, 4))
                if i >= 14:
                    return list(range(12, 16))
                return list(range(i - 2, i + 3))
            IR = []
            for j in range(16):
                ii = [i for i in range(16) if j in js_of(i)]
                IR.append((ii[0], ii[-1]))

            def tile_idx(j, i):
                return (j - i + 3) if i in (0, 1, 14, 15) else (9 - j + i)
            ptl_rot = Rot(A, "nptl", 4, [128, 768], BF16)
            ptc_rot = Rot(A, "nptc", 2, [128, 2, 512], BF16)
            cunits = [(pr, e_, j) for pr in range(4) for e_ in range(2) for j in range(16)]
            cst = {}
            gcount = {"i": 0}
            gbank = {}

            def na_cS(ui):
                pr, e_, j = cunits[ui]
                pb_ = pr % 2
                h = pr * 2 + e_
                base = e_ * 64
                nb_ = nab[h % 2]
                nbr = "nab%d" % (h % 2)
                i0_, i1_ = IR[j]
                L = i1_ - i0_ + 1
                LA = min(L, 4)
                bset = ui % 2
                bA, bB = 2 * bset, 2 * bset + 1
                kq = kaT[pb_]
                qq = qaT[pb_]

                nb8 = nab8[h % 2]
                pt, ptr = ptl_rot.next()
                runs = []
                for i in range(i0_, i1_ + 1):
                    col = i - i0_
                    ti = tile_idx(j, i)
                    bk = bA if col < 4 else bB
                    if runs and runs[-1][3] == bk and runs[-1][1] + runs[-1][2] == ti and runs[-1][0] + runs[-1][2] == col:
                        runs[-1][2] += 1
                    else:
                        runs.append([col, ti, 1, bk])

                def mmS(e):
                    lhsT = kq[base:base + 64, 256 + j * 128:256 + (j + 1) * 128]
                    ins = e.matmul(banks[bA][:, 0:LA * 128], lhsT=lhsT, rhs=qq[base:base + 64, i0_ * 128:(i0_ + LA) * 128], start=True, stop=False)
                    if L > 4:
                        ins = e.matmul(banks[bB][:, 0:(L - 4) * 128], lhsT=lhsT, rhs=qq[base:base + 64, (i0_ + 4) * 128:(i1_ + 1) * 128], start=True, stop=False)
                    lastA = max(rn for rn, r_ in enumerate(runs) if r_[3] == bA)
                    lastB = max([rn for rn, r_ in enumerate(runs) if r_[3] == bB] or [-1])
                    for rn, (col, ti, n, bk) in enumerate(runs):
                        pc = col if bk == bA else col - 4
                        ins = e.matmul(banks[bk][:, pc * 128:(pc + n) * 128], lhsT=ident, rhs=nb8[:, ti:ti + n, :].rearrange("p a b -> p (a b)"),
                                       start=False, stop=(rn == lastA or rn == lastB))
                    return ins
                qres = sorted(set("qaT%d_%d" % (pb_, i // 4) for i in range(i0_, i1_ + 1)))
                P.add("pe", mmS, reads=["kaT%d_%d" % (pb_, 1 + j // 4), "nab8_%d" % (h % 2), "ident"] + qres, writes=["bank%d" % bA, "bank%d" % bB])
                P.add("act", lambda e: e.activation(out=pt[:, 0:LA * 128], in_=banks[bA][:, 0:LA * 128], func=AF.Exp, scale=0.125),
                      reads=["bank%d" % bA], writes=[ptr + "a"])
                pres = [ptr + "a"]
                if L > 4:
                    P.add("act", lambda e: e.activation(out=pt[:, 512:L * 128], in_=banks[bB][:, 0:(L - 4) * 128], func=AF.Exp, scale=0.125),
                          reads=["bank%d" % bB], writes=[ptr + "b"])
                    pres.append(ptr + "b")
                cst[ui] = (pt, pres)

            def na_ctx(pr, e_, g):
                pb_ = pr % 2
                base = e_ * 64
                kq = kaT[pb_]
                qq = qaT[pb_]
                ptc, ptcr = ptc_rot.next()
                for n in range(2):
                    bb = 6 + n
                    P.add("pe", lambda e, n=n, bb=bb: e.matmul(banks[bb][:, :], lhsT=kq[base:base + 64, n * 128:(n + 1) * 128],
                                                           rhs=qq[base:base + 64, g * 512:(g + 1) * 512], start=True, stop=True),
                          reads=["kaT%d_0" % pb_, "qaT%d_%d" % (pb_, g)], writes=["bank%d" % bb])
                    P.add("act", lambda e, n=n, bb=bb: e.activation(out=ptc[:, n, :], in_=banks[bb][:, :], func=AF.Exp, scale=0.125),
                          reads=["bank%d" % bb], writes=[ptcr + "_%d" % n])
                bO = 4 + gcount["i"] % 2
                gcount["i"] += 1
                gbank[(pr, e_, g)] = bO

                def mmO(e):
                    ins = None
                    for n in range(2):
                        ins = e.matmul(banks[bO][:, :], lhsT=vaug[pb_][:, n, e_, :], rhs=ptc[:, n, :], start=(n == 0), stop=False,
                                       skip_group_check=True)
                    return ins
                P.add("pe", mmO, reads=[ptcr + "_0", ptcr + "_1", "vaug%d_0" % pb_, "vaug%d_1" % pb_], writes=["bank%d" % bO])

            def na_cO(ui):
                pr, e_, j = cunits[ui]
                pb_ = pr % 2
                pt, ptr = cst.pop(ui)
                i0_, i1_ = IR[j]
                for g in range(i0_ // 4, i1_ // 4 + 1):
                    if max(0, 4 * g - 2) == j:
                        na_ctx(pr, e_, g)
                    ia = max(i0_, 4 * g)
                    ib = min(i1_, 4 * g + 3)
                    bO = gbank[(pr, e_, g)]
                    P.add("pe", lambda e, ia=ia, ib=ib, bO=bO, g=g: e.matmul(
                        banks[bO][:, (ia - 4 * g) * 128:(ib - 4 * g + 1) * 128], lhsT=vaug[pb_][:, 2 + j, e_, :],
                        rhs=pt[:, (ia - i0_) * 128:(ib - i0_ + 1) * 128], start=False, stop=False, skip_group_check=True),
                        reads=ptr + ["vaug%d_%d" % (pb_, 2 + j)], writes=["bank%d" % bO])
                    if j == min(15, 4 * g + 5):
                        orow = slice(0, 64) if e_ == 0 else slice(64, 128)
                        srow = slice(64, 128) if e_ == 0 else slice(0, 64)
                        rec, recr = rec_rot.next()
                        P.add("dve", lambda e, rec=rec, bO=bO, orow=orow, srow=srow: e.reciprocal(out=rec[orow, :], in_=banks[bO][srow, :]),
                              reads=["bank%d" % bO], writes=[recr])
                        P.add("dve", lambda e, rec=rec, bO=bO, orow=orow, g=g: e.tensor_tensor(
                            out=oT[orow, pr, g * 512:(g + 1) * 512], in0=banks[bO][orow, :], in1=rec[orow, :], op=ALU.mult),
                            reads=["bank%d" % bO, recr], writes=["oT_%d_%d_%d" % (pr, g, e_)])

            SKEW = 2
            if b == 0 and ADABG:
                wada2 = [A.alloc([128, 8, 512], BF16) for _ in range(2)]
            for st_ in pn_steps(p2_items, px_rot, pxn_rot):
                st_()
            na_proj(0)
            na_bias_load(0)
            pend_np = []
            for ui in range(len(cunits) + SKEW):
                if ui < len(cunits):
                    pr, e_, j = cunits[ui]
                    if j == 8 and pr * 2 + e_ + 1 < 8:
                        na_bias_load(pr * 2 + e_ + 1)
                    if e_ == 0 and j == 4 and pr + 1 < 4:
                        pend_np = na_proj_pieces(pr + 1)
                    if e_ == 1 and j == 15:
                        for pc_ in pend_np:
                            pc_()
                        pend_np = []
                    elif pend_np:
                        pend_np.pop(0)()
                    na_cS(ui)
                if ui >= SKEW:
                    na_cO(ui - SKEW)
            P.barrier()
            A.release(mk2)

            cqnT = A.alloc([128, 6, SEQ], BF16)
            ckvnT = A.alloc([128, 2, NTOK], BF16)
            KRT = A.alloc([128, NTOK], BF16)
            rope = A.alloc([128, 2, NTOK], F32)
            mk4 = A.mark()
            wlora = A.alloc([128, 8, 1024], BF16)
            wkr = A.alloc([128, 8, 2, 96], BF16)
            cn_rot = Rot(A, "lcn", 4, [128, 1024], BF16)
            rt_rot = Rot(A, "lrt", 2, [128, 2, 512], F32)
            dma("pool", wlora, wlora_d, writes=["wlora0", "wlora1"], chan="ld_wlora")
            dma("pool", wkr, wkr_d, writes=["wkr"], chan="ld_wkr")
            for cs in range(2):
                dma("sp", rope[64:96, cs, :], rope_d[cs], writes=["rope%d" % cs], chan="ld_rope%d" % cs)
            lst = {}

            def lora_A(t):
                lat = t >= 2
                yb = ((0, 1), (4, 5), (6, 7))[t % 3] if LORA3 else ((4, 5) if t % 2 == 0 else (6, 7))

                def mm(e):
                    ins = None
                    for hh in range(2):
                        if hh == 0 and not lat:
                            continue
                        for k in range(8):
                            ins = e.matmul(banks[yb[hh]][:, :], lhsT=hT2[:, k, t * 128:(t + 1) * 128], rhs=wlora[:, k, hh * 512:(hh + 1) * 512],
                                           start=(k == 0), stop=(k == 7))
                    return ins
                P.add("pe", mm, reads=["hT2_%d" % t, "wlora0", "wlora1"], writes=["bank%d" % yb[0], "bank%d" % yb[1]])
                ss, ssr = ss_rot.next()
                rs, rsr = rs_rot.next()
                if lat:
                    P.add("act", lambda e: e.activation(out=junk[:, 0:512], in_=banks[yb[0]][:, :], func=AF.Square, accum_out=ss[:, 0:1]),
                          reads=["bank%d" % yb[0]], writes=[ssr + "_0", "junk"])
                    P.add("act", lambda e: e.activation(out=junk[:, 512:768], in_=banks[yb[1]][:, 0:256], func=AF.Square, accum_out=ss[:, 1:2]),
                          reads=["bank%d" % yb[1]], writes=[ssr + "_1", "junk"])
                P.add("act", lambda e: e.activation(out=junk[:, 768:1024], in_=banks[yb[1]][:, 256:512], func=AF.Square, accum_out=ss[:, 2:3]),
                      reads=["bank%d" % yb[1]], writes=[ssr + "_2", "junk"])
                if lat:
                    P.add("dve", lambda e: e.tensor_scalar(out=rs[:, 0:1], in0=ss[:, 0:1], scalar1=ss[:, 1:2], scalar2=1.0 / 768,
                                                           op0=ALU.add, op1=ALU.mult), reads=[ssr + "_0", ssr + "_1"], writes=[rsr + "q"])
                    P.add("dve", lambda e: e.tensor_scalar(out=rs[:, 0:1], in0=rs[:, 0:1], scalar1=EPS, scalar2=None,
                                                           op0=ALU.add), reads=[rsr + "q"], writes=[rsr + "q"])
                P.add("dve", lambda e: e.tensor_scalar(out=rs[:, 1:2], in0=ss[:, 2:3], scalar1=1.0 / 256, scalar2=EPS,
                                                       op0=ALU.mult, op1=ALU.add), reads=[ssr + "_2"], writes=[rsr + "k"])
                c0_ = 0 if lat else 1
                P.add("pool", lambda e: e.tensor_tensor(out=ss[:, c0_:2], in0=rs[:, c0_:2], in1=neghalf[:, c0_:2], op=ALU.pow),
                      reads=([rsr + "q"] if lat else []) + [rsr + "k", "neghalf", ssr + "_0", ssr + "_1"], writes=[ssr + "p"])
                lst[t] = (yb, ss, ssr, lat)

            lst2 = {}

            def lora_Bc(t):
                yb, ss, ssr, lat = lst.pop(t)
                cn, cnr = cn_rot.next()
                lst2[t] = (cn, cnr, lat)
                if lat:
                    P.add("act", lambda e: e.activation(out=cn[:, 0:512], in_=banks[yb[0]][:, :], func=AF.Copy, scale=ss[:, 0:1]),
                          reads=["bank%d" % yb[0], ssr + "p"], writes=[cnr + "_0"])
                    P.add("act", lambda e: e.activation(out=cn[:, 512:768], in_=banks[yb[1]][:, 0:256], func=AF.Copy, scale=ss[:, 0:1]),
                          reads=["bank%d" % yb[1], ssr + "p"], writes=[cnr + "_1"])
                P.add("act", lambda e: e.activation(out=cn[:, 768:1024], in_=banks[yb[1]][:, 256:512], func=AF.Copy, scale=ss[:, 1:2]),
                      reads=["bank%d" % yb[1], ssr + "p"], writes=[cnr + "_2"])

            def lora_Bt(t):
                cn, cnr, lat = lst2.pop(t)
                bi = 2 + (t % 2)
                pb = bankb[bi]
                k0 = 0 if lat else 6

                def tr(e):
                    ins = None
                    for k in range(k0, 8):
                        ins = e.transpose(out=pb[:, k * 128:(k + 1) * 128], in_=cn[:, k * 128:(k + 1) * 128], identity=ident)
                    return ins
                P.add("pe", tr, reads=([cnr + "_0", cnr + "_1"] if lat else []) + [cnr + "_2", "ident"], writes=["bank%d" % bi])
                if lat:
                    for k in range(6):
                        P.add("dve", lambda e, k=k: e.tensor_scalar(
                            out=cqnT[:, k, (t - 2) * 128:(t - 1) * 128], in0=pb[:, k * 128:(k + 1) * 128], scalar1=gqc[:, k:k + 1], scalar2=None,
                            op0=ALU.mult), reads=["bank%d" % bi, "gqc"], writes=["cqnT_%d" % (t - 2)])
                for k in range(2):
                    P.add("dve", lambda e, k=k: e.tensor_scalar(
                        out=ckvnT[:, k, t * 128:(t + 1) * 128], in0=pb[:, (6 + k) * 128:(7 + k) * 128], scalar1=gkvc[:, k:k + 1], scalar2=None,
                        op0=ALU.mult), reads=["bank%d" % bi, "gkvc"], writes=["ckvnT_%d" % t])

            for t in range(20):
                if t < 18:
                    lora_A(t)
                if 1 <= t <= 18:
                    lora_Bc(t - 1)
                if t >= 2:
                    lora_Bt(t - 2)
            c0 = 0
            ui = 0
            while c0 < NTOK:
                cw = 256 if c0 == 0 else 512
                bA, bB = ui % 2, 2 + ui % 2
                ui += 1
                tl = [(c0 + q) // 128 for q in range(0, cw, 128)]

                def mm(e, bA=bA, bB=bB, c0=c0, cw=cw):
                    ins = None
                    for wi, bb in ((0, bA), (1, bB)):
                        for k in range(8):
                            ins = e.matmul(banks[bb][0:96, 0:cw], lhsT=wkr[:, k, wi, :], rhs=hT2[:, k, c0:c0 + cw], start=(k == 0), stop=(k == 7))
                    return ins
                P.add("pe", mm, reads=["wkr"] + ["hT2_%d" % t for t in tl], writes=["bank%d" % bA, "bank%d" % bB])
                rt, rtr = rt_rot.next()
                P.add("dve", lambda e, rt=rt, bA=bA, c0=c0, cw=cw: e.tensor_tensor(out=rt[64:96, 0, 0:cw], in0=banks[bA][64:96, 0:cw], in1=rope[64:96, 0, c0:c0 + cw], op=ALU.mult),
                      reads=["bank%d" % bA, "rope0"], writes=[rtr + "a"])
                P.add("dve", lambda e, rt=rt, bB=bB, c0=c0, cw=cw: e.tensor_tensor(out=rt[64:96, 1, 0:cw], in0=banks[bB][64:96, 0:cw], in1=rope[64:96, 1, c0:c0 + cw], op=ALU.mult),
                      reads=["bank%d" % bB, "rope1"], writes=[rtr + "b"])
                P.add("pool", lambda e, rt=rt, c0=c0, cw=cw: e.tensor_tensor(out=KRT[64:96, c0:c0 + cw], in0=rt[64:96, 0, 0:cw], in1=rt[64:96, 1, 0:cw], op=ALU.add),
                      reads=[rtr + "a", rtr + "b"], writes=["KRT"])
                c0 += cw
            P.barrier()
            A.release(mk4)
            A.free_top()

            wkn = A.alloc([128, 2, 8, 64], BF16)
            wv = A.alloc([128, 2, 8, 64], BF16)
            wuq = [A.alloc([128, 6, 2, 96], BF16) for _ in range(2)]
            QT = [A.alloc([128, SEQ], BF16) for _ in range(2)]
            KT = [A.alloc([128, NTOK], BF16) for _ in range(2)]
            vaugm = [A.alloc([128, 18, 128], BF16) for _ in range(2)]
            PT = [A.alloc([128, 18, 512], BF16) for _ in range(2)]
            rt_rot = Rot(A, "mrt", 2, [128, 2, 512], F32)
            rec_rot = Rot(A, "mrec", 2, [128, 512], F32)
            raw_rot = Rot(A, "mraw", 2, [128, 512], BF16)
            permp = A.alloc([128, 96], BF16)
            dma("pool", permp[64:96, :], permp_d, writes=["permp"], chan="ld_permp")
            dma("pool", wkn, wkn_d, writes=["wkn"], chan="ld_wkn")
            dma("pool", wv, wv_d, writes=["wv"], chan="ld_wv")
            for i2 in range(2):
                P.add("pool", lambda e, i2=i2: e.memset(vaugm[i2], 1.0), writes=["vaugm%d" % i2])
            SC = float(96 ** -0.5)
            uim = {"i": 0}

            def mla_proj_pieces(h):
                hb = h % 2
                pieces = []

                def p_load():
                    dma("pool", wuq[hb], wuq_d[h], writes=["wuq%d" % hb], chan="ld_wuq%d" % hb)
                pieces.append(p_load)
                for cg in range(4):
                    bA, bB = 6, 7
                    st_ = {}

                    def p_a(cg=cg, st_=st_):
                        def mm(e):
                            ins = None
                            for k in range(6):
                                ins = e.matmul(banks[bA][0:96, :], lhsT=wuq[hb][:, k, 0, :], rhs=cqnT[:, k, cg * 512:(cg + 1) * 512], start=(k == 0), stop=(k == 5))
                            return ins
                        P.add("pe", mm, reads=["wuq%d" % hb] + ["cqnT_%d" % t for t in range(cg * 4, cg * 4 + 4)], writes=["bank%d" % bA])
                        qr = "QT%d_%d" % (hb, cg)
                        P.add("dve", lambda e: e.tensor_copy(out=QT[hb][0:64, cg * 512:(cg + 1) * 512], in_=banks[bA][0:64, :]),
                              reads=["bank%d" % bA], writes=[qr + "n"])
                        rw, rwr = raw_rot.next()
                        P.add("dve", lambda e: e.tensor_copy(out=rw[64:96, :], in_=banks[bA][64:96, :]), reads=["bank%d" % bA], writes=[rwr])
                        rt, rtr = rt_rot.next()
                        c0 = 256 + cg * 512
                        P.add("dve", lambda e: e.tensor_tensor(out=rt[64:96, 0, :], in0=banks[bA][64:96, :], in1=rope[64:96, 0, c0:c0 + 512], op=ALU.mult),
                              reads=["bank%d" % bA, "rope0"], writes=[rtr + "a"])
                        st_["v"] = (rw, rwr, rt, rtr, c0, qr)

                    def p_b(cg=cg, st_=st_):
                        rw, rwr, rt, rtr, c0, qr = st_["v"]
                        P.add("pe", lambda e: e.matmul(banks[bB][0:96, :], lhsT=permp[64:96, :], rhs=rw[64:96, :], start=True, stop=True),
                              reads=[rwr, "permp"], writes=["bank%d" % bB])
                        P.add("dve", lambda e: e.tensor_tensor(out=rt[64:96, 1, :], in0=banks[bB][64:96, :], in1=rope[64:96, 1, c0:c0 + 512], op=ALU.mult),
                              reads=["bank%d" % bB, "rope1"], writes=[rtr + "b"])
                        P.add("pool", lambda e: e.tensor_tensor(out=QT[hb][64:96, cg * 512:(cg + 1) * 512], in0=rt[64:96, 0, :], in1=rt[64:96, 1, :], op=ALU.add),
                              reads=[rtr + "a", rtr + "b"], writes=[qr + "r"])
                    pieces.append(p_a)
                    pieces.append(p_b)
                c0 = 0
                while c0 < NTOK:
                    cw = 256 if c0 == 0 else 512

                    def p_k(c0=c0, cw=cw):
                        bb = 6 + uim["i"] % 2
                        uim["i"] += 1
                        tl = [(c0 + q) // 128 for q in range(0, cw, 128)]

                        def mm(e):
                            ins = None
                            for k in range(2):
                                ins = e.matmul(banks[bb][0:64, 0:cw], lhsT=wkn[:, k, h, :], rhs=ckvnT[:, k, c0:c0 + cw], start=(k == 0), stop=(k == 1))
                            return ins
                        P.add("pe", mm, reads=["wkn"] + ["ckvnT_%d" % t for t in tl], writes=["bank%d" % bb])
                        P.add("dve", lambda e: e.tensor_copy(out=KT[hb][0:64, c0:c0 + cw], in_=banks[bb][0:64, 0:cw]),
                              reads=["bank%d" % bb], writes=["KT%d_n%d" % (hb, c0)])
                        if c0 == 0:
                            P.add("pool", lambda e: e.tensor_copy(out=KT[hb][64:96, :], in_=KRT[64:96, :]), reads=["KRT"], writes=["KT%d_r" % hb])
                    pieces.append(p_k)
                    c0 += cw
                for t8 in range(0, 18, 8):
                    def p_v(t8=t8):
                        tn = min(8, 18 - t8)
                        bb = 6 + uim["i"] % 2
                        uim["i"] += 1

                        def mm(e):
                            ins = None
                            for q in range(tn):
                                for k in range(2):
                                    ins = e.matmul(banks[bb][:, q * 64:(q + 1) * 64], lhsT=ckvnT[:, k, (t8 + q) * 128:(t8 + q + 1) * 128], rhs=wv[:, k, h, :],
                                                   start=(k == 0), stop=(k == 1))
                            return ins
                        P.add("pe", mm, reads=["wv"] + ["ckvnT_%d" % (t8 + q) for q in range(tn)], writes=["bank%d" % bb])
                        P.add("dve", lambda e: e.tensor_copy(
                            out=vaugm[hb][:, t8:t8 + tn, hb * 64:(hb + 1) * 64], in_=banks[bb][:, 0:tn * 64].rearrange("p (a b) -> p a b", a=tn)),
                            reads=["bank%d" % bb], writes=["vaugm%d" % hb])
                    pieces.append(p_v)
                return pieces

            munits = [(h, qg) for h in range(8) for qg in range(4)]

            def mla_qk(ui, kc):
                h, qg = munits[ui]
                hb = h % 2
                ptb = PT[ui % 2]
                ptn = "PT%d" % (ui % 2)
                bb = uim["i"] % 4
                uim["i"] += 1
                ktres = ["KT%d_r" % hb] + ["KT%d_n%d" % (hb, c) for c in (0, 256, 768, 1280, 1792)]
                P.add("pe", lambda e: e.matmul(
                    banks[bb][:, :], lhsT=KT[hb][0:96, kc * 128:(kc + 1) * 128], rhs=QT[hb][0:96, qg * 512:(qg + 1) * 512], start=True, stop=True),
                    reads=ktres + ["QT%d_%dn" % (hb, qg), "QT%d_%dr" % (hb, qg)], writes=["bank%d" % bb])
                P.add("act", lambda e: e.activation(out=ptb[:, kc, :], in_=banks[bb][:, :], func=AF.Exp, scale=SC),
                      reads=["bank%d" % bb], writes=["%s_%d" % (ptn, kc)])

            def mla_pv(ui, kc):
                h, qg = munits[ui]
                hb = h % 2
                ptb = PT[ui % 2]
                ptn = "PT%d" % (ui % 2)
                bO = 4 + ui % 2
                P.add("pe", lambda e: e.matmul(banks[bO][:, :], lhsT=vaugm[hb][:, kc, :], rhs=ptb[:, kc, :], start=(kc == 0), stop=(kc == 17)),
                      reads=["%s_%d" % (ptn, kc), "vaugm%d" % hb], writes=["bank%d" % bO])

            def mla_fin(ui):
                h, qg = munits[ui]
                hb = h % 2
                bO = 4 + ui % 2
                orow = slice(0, 64) if hb == 0 else slice(64, 128)
                srow = slice(64, 128) if hb == 0 else slice(0, 64)
                rec, recr = rec_rot.next()
                P.add("dve", lambda e: e.reciprocal(out=rec[orow, :], in_=banks[bO][srow, :]), reads=["bank%d" % bO], writes=[recr])
                P.add("dve", lambda e: e.tensor_tensor(out=oT[orow, 4 + h // 2, qg * 512:(qg + 1) * 512], in0=banks[bO][orow, :], in1=rec[orow, :], op=ALU.mult),
                      reads=["bank%d" % bO, recr], writes=["oT_%d_%d_%d" % (4 + h // 2, qg, hb)])

            for pc_ in mla_proj_pieces(0):
                pc_()
            NU = len(munits)
            pend_p = []
            stepc = 0
            for ui in range(NU + 1):
                if ui < NU and munits[ui][1] == 0:
                    for pc_ in pend_p:
                        pc_()
                    pend_p = []
                    if munits[ui][0] + 1 < 8:
                        pend_p = mla_proj_pieces(munits[ui][0] + 1)
                for kc in range(18):
                    if ui < NU:
                        mla_qk(ui, kc)
                    if ui >= 1:
                        mla_pv(ui - 1, kc)
                    stepc += 1
                    if pend_p and stepc % 4 == 2:
                        pend_p.pop(0)()
                if ui >= 1:
                    mla_fin(ui - 1)
            P.barrier()
            A.release(mk2)

            A.release(mkb)
            oT_keep = A.alloc([128, 8, SEQ], BF16)
            wgate = A.alloc([128, 8, 2, 8, 128], BF16)
            wo = A.alloc([128, 8, 2, 4, 128], BF16)
            wout = A.alloc([128, 8, D], BF16)
            C3 = A.alloc([128, D], F32)
            hTg = [A.alloc([128, 8, 512], BF16) for _ in range(2)]
            yT = [A.alloc([128, 8, 512], BF16) for _ in range(2)]
            x_rot = Rot(A, "mx", 3, [128, D], F32)
            xm_rot = Rot(A, "mxd", 3, [128, D], F32)
            mpend = [None]
            xn_rot = Rot(A, "mxn", 3, [128, D], BF16)
            yt_rot = Rot(A, "myt", 2, [128, D], F32)
            sg_rot = Rot(A, "msg", 4, [128, 512], F32)
            y12_rot = Rot(A, "my12", 4, [128, 512], F32)
            for dc in range(8):
                dma("pool", wgate[:, dc], wgate_d[dc], writes=["wgate%d" % dc], chan="ld_wm%d" % dc)
                dma("pool", wo[:, dc], wo_d[dc], writes=["wo%d" % dc], chan="ld_wn%d" % dc)
            dma("pool", wout, wout_d, writes=["wout0", "wout1"], chan="ld_wout")
            make_bc(C3, "C3", Cc[:, b, 1, :])
            ui = 0
            trb["banks"] = (4, 5, 6, 7)

            def merge_pn_steps(g):
                hg = hTg[g % 2]
                items = []
                for tt in range(4):
                    t = 2 + g * 4 + tt
                    items.append((x1s[t * 128:(t + 1) * 128, :], "x1s_%d" % t, ABc[:, b, 1, 0, :], ABc[:, b, 1, 1, :],
                                  (lambda k, tt=tt, hg=hg: hg[:, k, tt * 128:(tt + 1) * 128]), "hTg%d_%d" % (g % 2, tt)))
                return pn_steps(items, x_rot, xn_rot)
            for g in range(4):
                hg = hTg[g % 2]
                yg = yT[g % 2]
                if g == 0:
                    for st_ in merge_pn_steps(0):
                        st_()
                hres = ["hTg%d_%d" % (g % 2, tt) for tt in range(4)]
                nxt_steps = merge_pn_steps(g + 1) if g + 1 < 4 else []
                for dc in range(8):
                    bs = [0, 1, 2, 3]
                    ui += 1
                    if 1 <= dc <= 6 and nxt_steps:
                        nxt_steps[dc - 1]()
                    for ab in range(2):
                        def mmg(e, ab=ab, dc=dc, hg=hg):
                            ins = None
                            for k in range(8):
                                ins = e.matmul(banks[bs[ab]][:, :], lhsT=wgate[:, dc, ab, k, :], rhs=hg[:, k, :], start=(k == 0), stop=(k == 7))
                            return ins
                        P.add("pe", mmg, reads=hres + ["wgate%d" % dc], writes=["bank%d" % bs[ab]])
                    for ab in range(2):
                        def mmz(e, ab=ab, dc=dc, g=g):
                            ins = None
                            for k in range(4):
                                ins = e.matmul(banks[bs[2 + ab]][:, :], lhsT=wo[:, dc, ab, k, :], rhs=oT_keep[:, ab * 4 + k, g * 512:(g + 1) * 512],
                                               start=(k == 0), stop=(k == 3))
                            return ins
                        ores = ["oT_%d_%d_%d" % (c, g, e_) for c in range(ab * 4, ab * 4 + 4) for e_ in range(2)]
                        P.add("pe", mmz, reads=ores + ["wo%d" % dc], writes=["bank%d" % bs[2 + ab]])
                    y12 = []
                    for ab in range(2):
                        sg, sgr = sg_rot.next()
                        yy, yyr = y12_rot.next()
                        P.add("act", lambda e, sg=sg, bq=bs[ab], ab=ab, dc=dc: e.activation(out=sg, in_=banks[bq][:, :], func=AF.Sigmoid, bias=bgc[:, ab, dc:dc + 1]),
                              reads=["bank%d" % bs[ab], "bgc"], writes=[sgr])
                        P.add("dve", lambda e, sg=sg, yy=yy, bq=bs[2 + ab]: e.tensor_tensor(out=yy, in0=sg, in1=banks[bq][:, :], op=ALU.mult),
                              reads=[sgr, "bank%d" % bs[2 + ab]], writes=[yyr])
                        y12.append((yy, yyr))
                    P.add("pool", lambda e, y12=y12, yg=yg, dc=dc: e.tensor_tensor(out=yg[:, dc, :], in0=y12[0][0], in1=y12[1][0], op=ALU.add),
                          reads=[y12[0][1], y12[1][1]], writes=["yT%d_%d" % (g % 2, dc)])
                for tt in range(4):
                    t = 2 + g * 4 + tt
                    yb = (4, 5) if tt % 2 == 0 else (6, 7)
                    xt, xr = xm_rot.next()
                    dma("sp", xt, x1s[t * 128:(t + 1) * 128, :], reads=["x1s_%d" % t], writes=[xr], chan="ld_" + xr)
                    if mpend[0] is not None:
                        pd = mpend[0]
                        dma("pool", pd[0], pd[1], reads=[pd[2]], writes=[pd[3]], chan="st_" + pd[2])
                        mpend[0] = None
                    for hh in range(2):
                        def mm(e, hh=hh, yb=yb, tt=tt, yg=yg):
                            ins = None
                            for k in range(8):
                                ins = e.matmul(banks[yb[hh]][:, :], lhsT=yg[:, k, tt * 128:(tt + 1) * 128], rhs=wout[:, k, hh * 512:(hh + 1) * 512],
                                               start=(k == 0), stop=(k == 7))
                            return ins
                        P.add("pe", mm, reads=["yT%d_%d" % (g % 2, dc) for dc in range(8)] + ["wout%d" % hh], writes=["bank%d" % yb[hh]])
                    epilogue(yb, xt, xr, C3, "C3", yt_rot)
                    tl = g * 4 + tt
                    dma("pool", x2s[tl * 128:(tl + 1) * 128, :], xt, reads=[xr], writes=["x2s_%d" % tl], chan="st_" + xr)
            trb["banks"] = (2, 3)
            P.barrier()
            A.release(mkb)

            g_l = []
            for g in range(2):
                g_l.append((b, [(x2s[t * 128:(t + 1) * 128, :], out_d[b, t * 128:(t + 1) * 128, :], "x2s_%d" % t, "out_%d_%d" % (b, t), b)
                                for t in range(g * 8, g * 8 + 8)]))
            ffn_phase(1, g_l, 2, 2)

        P.emit()
        print("arena peak bytes:", A.peak, "ops:", {e: len(P.ops[e]) for e in ENGS})
    return nc


def _partner():
    p = np.zeros(32, np.int64)
    for d in range(32):
        ax, dd = d // 16, d % 16
        p[d] = ax * 16 + (dd + 8 if dd < 8 else dd - 8)
    return p


def _rope_table():
    half = 8
    freqs = (np.float32(10000.0) ** (-np.arange(half, dtype=np.float32) / np.float32(half))).astype(np.float32)
    t = np.arange(SEQ)
    rows = (t // 64).astype(np.float32)
    cols = (t % 64).astype(np.float32)
    ang_r = rows[:, None] * freqs
    ang_c = cols[:, None] * freqs
    tab = np.zeros((2, 32, NTOK), np.float32)
    tab[0, :, :CTX] = 1.0
    for d in range(32):
        ax, dd = d // 16, d % 16
        ang = (ang_r if ax == 0 else ang_c)[:, dd % 8].astype(np.float32)
        tab[0, d, CTX:] = np.cos(ang)
        tab[1, d, CTX:] = (-np.sin(ang)) if dd < 8 else np.sin(ang)
    return tab


def _na_bias(rpb):
    kc = np.arange(64)
    qc = np.arange(64)
    ws = np.clip(qc - 8, 0, 48)
    colvalid = (kc[:, None] >= ws[None, :]) & (kc[:, None] < ws[None, :] + 16)
    coloff = np.clip(kc[:, None] - qc[None, :], -15, 15) + 15
    negt = np.full((8, 64, 64), NEG, np.float32)

    def tile(dc, middle):
        out = np.full((8, 2, 64, 2, 64), NEG, np.float32)
        for kr in range(2):
            for qr in range(2):
                dr = 2 * dc + kr - qr
                ri = dr + 7
                if ri < 0 or ri > 14:
                    continue
                if middle and not (-4 <= dr <= 3):
                    continue
                vals = rpb[:, ri, :][:, coloff]
                out[:, kr, :, qr, :] = np.where(colvalid[None], vals, negt)
        return out.reshape(8, 128, 128)
    tiles = [tile(-3, False), tile(-2, False), tile(-1, True), tile(0, True), tile(1, True), tile(2, False), tile(3, False),
             tile(2, True), tile(1, True), tile(0, True), tile(-1, True), tile(-2, True)]
    return np.ascontiguousarray(np.stack(tiles, 1).transpose(0, 2, 1, 3))


def _prep_shared(inp):
    f = lambda a: np.ascontiguousarray(a, dtype=np.float32)
    sh = {}
    w_ada = inp["w_ada"][0]
    sh["w_ada_t"] = f(w_ada.reshape(8, 128, 18, 512).transpose(2, 1, 0, 3))
    sh["b_ada_cols"] = f(inp["b_ada"][0].reshape(72, 128).T)
    sh["norm_g_cols"] = f(inp["norm_g"][0].reshape(6, 8, 128).transpose(2, 0, 1))
    for n, p in (("f1", "ffn1"), ("f2", "ffn2")):
        a = np.stack([inp[p + "_w1"][0], inp[p + "_w3"][0]], 0)
        sh[n + "_w13"] = f(a.reshape(2, 8, 128, 22, 128).transpose(3, 2, 0, 1, 4))
        sh[n + "_w2"] = f(inp[p + "_w2"][0].reshape(11, 2, 128, D).transpose(0, 2, 1, 3))
    w_in = inp["w_in"][0]
    qa = w_in[:, 0:512].reshape(8, 128, 4, 128)
    ka = w_in[:, 512:1024].reshape(8, 128, 4, 128)
    sh["w_qk"] = f(np.stack([qa, ka], 0).transpose(3, 2, 0, 1, 4))
    sh["w_va"] = f(w_in[:, 1024:1536].reshape(8, 128, 4, 128).transpose(2, 1, 0, 3))
    sh["w_lora"] = f(w_in[:, 1536:2560].reshape(8, 128, 1024).transpose(1, 0, 2))
    part = _partner()
    kr = w_in[:, 2560:2592]
    wkr = np.zeros((128, 8, 2, 96), np.float32)
    wkr[:, :, 0, 64:96] = kr.reshape(8, 128, 32).transpose(1, 0, 2)
    wkr[:, :, 1, 64:96] = kr[:, part].reshape(8, 128, 32).transpose(1, 0, 2)
    sh["w_kr"] = wkr
    ga = w_in[:, 2592:3616]
    gb = w_in[:, 3616:4640]
    sh["w_gate"] = f(np.stack([ga, gb], 0).reshape(2, 8, 128, 8, 128).transpose(3, 2, 0, 1, 4))
    sh["b_gate_cols"] = f(inp["b_gate"][0].reshape(2, 8, 128).transpose(2, 0, 1))
    sh["g_q_cols"] = f(inp["g_q_lora"][0].reshape(6, 128).T)
    sh["g_kv_cols"] = f(inp["g_kv_lora"][0].reshape(2, 128).T)
    w_uq = inp["w_uq"][0]
    wuq = np.zeros((8, 128, 6, 2, 96), np.float32)
    for h in range(8):
        a_ = w_uq[:, h * 96:(h + 1) * 96]
        wuq[h, :, :, 0, :] = a_.reshape(6, 128, 96).transpose(1, 0, 2)
        r_ = w_uq[:, h * 96 + 64 + part]
        wuq[h, :, :, 1, 64:96] = r_.reshape(6, 128, 32).transpose(1, 0, 2)
    sh["w_uq_t"] = wuq
    w_ukv = inp["w_ukv"][0].reshape(2, 128, 8, 128)
    sh["w_kn"] = f(w_ukv[:, :, :, 0:64].transpose(1, 0, 2, 3))
    sh["w_v"] = f(w_ukv[:, :, :, 64:128].transpose(1, 0, 2, 3))
    sh["rope_tab"] = _rope_table()
    sh["na_bias"] = _na_bias(np.asarray(inp["rpb"][0], np.float32))
    wo = np.stack([inp["w_o_na"][0], inp["w_o_mla"][0]], 0)
    sh["w_o"] = f(wo.reshape(2, 4, 128, 8, 128).transpose(3, 2, 0, 1, 4))
    sh["w_out_t"] = f(inp["w_out"][0].reshape(8, 128, D).transpose(1, 0, 2))
    sh["ident"] = np.eye(128, dtype=np.float32)
    pp = np.zeros((32, 96), np.float32)
    for m in range(32):
        pp[part[m], 64 + m] = 1.0
    sh["permp"] = pp
    return sh


def _in_maps(inp):
    inp = {k: np.asarray(v) for k, v in inp.items()}
    sh = _prep_shared(inp)
    maps = []
    for c in range(NCORES):
        m = dict(sh)
        b0 = c * NB
        m["x"] = np.ascontiguousarray(inp["x"][b0:b0 + NB], dtype=np.float32)
        m["ctx"] = np.ascontiguousarray(inp["ctx"][b0:b0 + NB], dtype=np.float32)
        cs = np.stack([inp["c"][b0], inp["c"][b0 + 1], inp["c_ctx"]], 0)
        m["ccols"] = np.ascontiguousarray(cs.reshape(3, 8, 128).transpose(2, 1, 0), dtype=np.float32)
        maps.append(m)
    return maps


_NC_CACHE = {}


def kernel(**inputs):
    if "nc" not in _NC_CACHE:
        _NC_CACHE["nc"] = build_program()
    nc = _NC_CACHE["nc"]
    maps = _in_maps(inputs)
    res = run_bass_kernel_spmd(nc, maps, core_ids=list(range(NCORES)))
    out = np.concatenate([np.asarray(r["out"]) for r in res.results], axis=0)
    return out.astype(np.float32)
```
